# Optimizing a Trainium2 kernel written in Bass

```python
import math
import jax, jax.numpy as jnp
from jax import lax
import numpy as np

D_MODEL = 1024
BATCH = 16
SEQ = 2048
DEPTH = 1

CHUNK = 64
N_META = 16
ATTN_HEADS = 8
ATTN_HEAD_DIM = 64
ATTN_WIDTH = ATTN_HEADS * ATTN_HEAD_DIM
Q_BLOCK = 128
SSM_WIDTH = 512
SSM_GROUP = 16
SSM_GROUPS = SSM_WIDTH // SSM_GROUP
SSM_STATE = 64
D_FF = 2816
CONV_WIDTH = 3
RMS_EPS = 1e-6
SPLITS = (ATTN_WIDTH, 2 * ATTN_WIDTH, 3 * ATTN_WIDTH, 3 * ATTN_WIDTH + ATTN_HEADS,
          3 * ATTN_WIDTH + ATTN_HEADS + SSM_WIDTH)
IN_WIDTH = 3 * ATTN_WIDTH + ATTN_HEADS + SSM_WIDTH + 2 * D_MODEL

kernel_name = "hybrid_fox_s5_convffn_meta"


def _rmsnorm(x, g):
    xf = x.astype(jnp.float32)
    y = xf * lax.rsqrt(jnp.mean(xf * xf, axis=-1, keepdims=True) + RMS_EPS)
    return (y * g.astype(jnp.float32)).astype(x.dtype)


def _forgetting_attention(q, k, v, f_logit, b_forget):
    bsz, seq_len, _ = q.shape
    h, dh = ATTN_HEADS, ATTN_HEAD_DIM
    log_f = jax.nn.log_sigmoid(f_logit.astype(jnp.float32) + b_forget.astype(jnp.float32))
    cum_f = jnp.cumsum(log_f, axis=1)
    lp = -(-seq_len // Q_BLOCK) * Q_BLOCK
    pad = lp - seq_len
    q = jnp.pad(q.reshape(bsz, seq_len, h, dh), ((0, 0), (0, pad), (0, 0), (0, 0)))
    k = jnp.pad(k.reshape(bsz, seq_len, h, dh), ((0, 0), (0, pad), (0, 0), (0, 0)))
    v = jnp.pad(v.reshape(bsz, seq_len, h, dh), ((0, 0), (0, pad), (0, 0), (0, 0)))
    cum_f = jnp.pad(cum_f, ((0, 0), (0, pad), (0, 0)))
    nb = lp // Q_BLOCK
    q_blocks = jnp.moveaxis(q.reshape(bsz, nb, Q_BLOCK, h, dh), 1, 0)
    fq_blocks = jnp.moveaxis(cum_f.reshape(bsz, nb, Q_BLOCK, h), 1, 0)
    q_pos = jnp.arange(lp, dtype=jnp.int32).reshape(nb, Q_BLOCK)
    k_pos = jnp.arange(lp, dtype=jnp.int32)
    fk = jnp.transpose(cum_f, (0, 2, 1))[:, :, None, :]
    scale = 1.0 / math.sqrt(dh)

    def block(args):
        qb, fqb, qp = args
        s = jnp.einsum('bqhd,bkhd->bhqk', qb, k, preferred_element_type=jnp.float32) * scale
        s = s + jnp.transpose(fqb, (0, 2, 1))[..., None] - fk
        mask = k_pos[None, :] <= qp[:, None]
        s = jnp.where(mask[None, None], s, -1e30)
        p = jax.nn.softmax(s, axis=-1)
        return jnp.einsum('bhqk,bkhd->bqhd', p.astype(v.dtype), v)

    out = lax.map(block, (q_blocks, fq_blocks, q_pos))
    out = jnp.moveaxis(out, 0, 1).reshape(bsz, lp, ATTN_WIDTH)
    return out[:, :seq_len]


def _s5(u, lam_re, lam_im, b_re, b_im, c_re, c_im, d_skip, log_dt):
    bsz, seq_len, _ = u.shape
    f32 = jnp.float32
    uf = u.astype(f32)
    ug = uf.reshape(bsz, seq_len, SSM_GROUPS, SSM_GROUP)
    lam = lax.complex(lam_re.astype(f32), lam_im.astype(f32))
    dt = jnp.exp(log_dt.astype(f32))[:, None]
    lam_bar = jnp.exp(lam * dt)
    bmat = lax.complex(b_re.astype(f32), b_im.astype(f32))
    cmat = lax.complex(c_re.astype(f32), c_im.astype(f32))
    b_bar = ((lam_bar - 1.0) / lam)[..., None] * bmat
    bu = jnp.einsum('gpc,blgc->lbgp', b_bar, ug.astype(jnp.complex64))
    a = jnp.broadcast_to(lam_bar, (seq_len, 1, SSM_GROUPS, SSM_STATE))

    def combine(e1, e2):
        a1, b1 = e1
        a2, b2 = e2
        return a1 * a2, a2 * b1 + b2

    _, states = lax.associative_scan(combine, (a, bu), axis=0)
    y = jnp.einsum('gcp,lbgp->blgc', cmat, states).real
    return y.reshape(bsz, seq_len, SSM_WIDTH) + d_skip.astype(f32) * uf


def _mixer(xn, w_in, b_forget, w_attn_out, lam_re, lam_im, b_re, b_im, c_re, c_im,
           d_skip, log_dt, w_glu, w_o):
    proj = xn @ w_in
    q, k, v, f_logit, u, gates = jnp.split(proj, SPLITS, axis=-1)
    y_a = _forgetting_attention(q, k, v, f_logit, b_forget) @ w_attn_out
    z = jax.nn.gelu(_s5(u, lam_re, lam_im, b_re, b_im, c_re, c_im, d_skip, log_dt)).astype(xn.dtype)
    zg = z @ w_glu
    y_s = zg[..., :D_MODEL] * jax.nn.sigmoid(zg[..., D_MODEL:])
    g = jax.nn.sigmoid(gates.astype(jnp.float32))
    merged = g[..., :D_MODEL] * y_a.astype(jnp.float32) + g[..., D_MODEL:] * y_s.astype(jnp.float32)
    return merged.astype(xn.dtype) @ w_o


def _conv_ffn(xn, w_up, conv_w, conv_b, w_down):
    seq_len = xn.shape[1]
    hid = xn @ w_up
    hp = jnp.pad(hid, ((0, 0), (CONV_WIDTH - 1, 0), (0, 0)))
    conv = conv_b + sum(hp[:, j:j + seq_len] * conv_w[j] for j in range(CONV_WIDTH))
    val, gate = jnp.split(conv, 2, axis=-1)
    return (jax.nn.silu(gate) * val) @ w_down


def setup_inputs(seed: int = 0) -> dict:
    key = jax.random.key(seed)
    ks = jax.random.split(key, 24)
    f32 = jnp.float32
    n = lambda k, shape, s: jax.random.normal(k, shape, f32) * s
    L = DEPTH
    log_dt = jax.random.uniform(ks[13], (L, SSM_GROUPS), f32, math.log(1e-3), math.log(1e-1))
    lam_im = jnp.broadcast_to(math.pi * jnp.arange(SSM_STATE, dtype=f32), (L, SSM_GROUPS, SSM_STATE))
    return {
        "x": jax.random.normal(ks[0], (BATCH, SEQ, D_MODEL), f32),
        "meta_tokens": n(ks[1], (N_META, D_MODEL), 1.0),
        "norm_mix_g": 1.0 + n(ks[2], (L, D_MODEL), 0.02),
        "w_in": n(ks[3], (L, D_MODEL, IN_WIDTH), D_MODEL ** -0.5),
        "b_forget": 2.0 + n(ks[4], (L, ATTN_HEADS), 0.5),
        "w_attn_out": n(ks[5], (L, ATTN_WIDTH, D_MODEL), ATTN_WIDTH ** -0.5),
        "ssm_lambda_re": -0.5 + n(ks[6], (L, SSM_GROUPS, SSM_STATE), 0.01),
        "ssm_lambda_im": lam_im + n(ks[7], (L, SSM_GROUPS, SSM_STATE), 0.01),
        "ssm_b_re": n(ks[8], (L, SSM_GROUPS, SSM_STATE, SSM_GROUP), (2 * SSM_GROUP) ** -0.5),
        "ssm_b_im": n(ks[9], (L, SSM_GROUPS, SSM_STATE, SSM_GROUP), (2 * SSM_GROUP) ** -0.5),
        "ssm_c_re": n(ks[10], (L, SSM_GROUPS, SSM_GROUP, SSM_STATE), SSM_STATE ** -0.5),
        "ssm_c_im": n(ks[11], (L, SSM_GROUPS, SSM_GROUP, SSM_STATE), SSM_STATE ** -0.5),
        "ssm_d": n(ks[12], (L, SSM_WIDTH), 1.0),
        "ssm_log_dt": log_dt,
        "w_glu": n(ks[14], (L, SSM_WIDTH, 2 * D_MODEL), SSM_WIDTH ** -0.5),
        "w_o": n(ks[15], (L, D_MODEL, D_MODEL), D_MODEL ** -0.5),
        "norm_ffn_g": 1.0 + n(ks[16], (L, D_MODEL), 0.02),
        "w_ffn_up": n(ks[17], (L, D_MODEL, 2 * D_FF), D_MODEL ** -0.5),
        "ffn_conv_w": n(ks[18], (L, CONV_WIDTH, 2 * D_FF), CONV_WIDTH ** -0.5),
        "ffn_conv_b": n(ks[19], (L, 2 * D_FF), 0.01),
        "w_ffn_down": n(ks[20], (L, D_FF, D_MODEL), D_FF ** -0.5),
        "norm_final_g": 1.0 + n(ks[21], (D_MODEL,), 0.02),
    }


def reference(x, meta_tokens, norm_mix_g, w_in, b_forget, w_attn_out, ssm_lambda_re,
              ssm_lambda_im, ssm_b_re, ssm_b_im, ssm_c_re, ssm_c_im, ssm_d, ssm_log_dt,
              w_glu, w_o, norm_ffn_g, w_ffn_up, ffn_conv_w, ffn_conv_b, w_ffn_down,
              norm_final_g):
    bsz = x.shape[0]
    meta = jnp.broadcast_to(meta_tokens.astype(x.dtype)[None], (bsz, N_META, x.shape[-1]))
    h = jnp.concatenate([meta, x], axis=1)
    for layer in range(DEPTH):
        xn = _rmsnorm(h, norm_mix_g[layer])
        h = h + _mixer(xn, w_in[layer], b_forget[layer], w_attn_out[layer],
                       ssm_lambda_re[layer], ssm_lambda_im[layer], ssm_b_re[layer],
                       ssm_b_im[layer], ssm_c_re[layer], ssm_c_im[layer], ssm_d[layer],
                       ssm_log_dt[layer], w_glu[layer], w_o[layer])
        xn = _rmsnorm(h, norm_ffn_g[layer])
        h = h + _conv_ffn(xn, w_ffn_up[layer], ffn_conv_w[layer], ffn_conv_b[layer],
                          w_ffn_down[layer])
    h = _rmsnorm(h, norm_final_g)
    return h[:, N_META:]
```

```python
import math
import os
import contextlib
DBG = os.environ.get('DBG', '')
import numpy as np
import concourse.bass as bass
import concourse.mybir as mybir
from concourse.bass_utils import run_bass_kernel_spmd

F32 = mybir.dt.float32
BF16 = mybir.dt.bfloat16
I32 = mybir.dt.int32
AF = mybir.ActivationFunctionType
ALU = mybir.AluOpType

D = 1024
KT = 8
H = 8
NMETA = 16
SEQ = 2048
L = NMETA + SEQ
DFF = 2816
NFT = 22
TCH = 8
NG = 32
NSTRIP = 16
EPS = 1e-6
NSEQ = 2
IN_W = 4104
MASKV = -30000.0


class Buf:
    __slots__ = ("writers", "readers", "excl")

    def __init__(self, excl=False):
        self.writers = {}
        self.readers = {}
        self.excl = excl


class Prog:
    ENG = ("pe", "act", "dve", "pool", "sp")

    def __init__(self, nc, n_streams=16):
        self.nc = nc
        self.ops = {e: [] for e in self.ENG}
        self.cnt = {e: 0 for e in self.ENG}
        self.n_streams = n_streams
        self.stream_cnt = [0] * n_streams
        self.waited = {e: {} for e in self.ENG}
        self.groups = {"sp": list(range(0, 8)), "pool": list(range(8, 12)), "cast": list(range(12, 16))}
        self.rr = {"sp": 0, "pool": 0, "cast": 0}

    def _deps(self, eng, reads, writes, skip_pe=False):
        deps = {}

        def add(d):
            if d is None:
                return
            k, v = d
            if deps.get(k, 0) < v:
                deps[k] = v
        for b in reads:
            for d in b.writers.items():
                add(d)
        for b in writes:
            for d in b.writers.items():
                add(d)
            for d in b.readers.items():
                add(d)
        waits = []
        w = self.waited[eng]
        for k, v in deps.items():
            if skip_pe and k == "pe":
                continue
            if w.get(k, 0) >= v:
                continue
            w[k] = v
            waits.append((k, v))
        return waits

    def _commit(self, me, reads, writes):
        k, v = me
        for b in reads:
            if b.readers.get(k, 0) < v:
                b.readers[k] = v
        for b in writes:
            if b.writers.get(k, 0) < v:
                b.writers[k] = v

    def op(self, eng, fn, reads=(), writes=(), pe_mm=False):
        ex = [b for b in reads if b.excl]
        if ex:
            writes = list(writes) + ex
        waits = self._deps(eng, reads, writes, skip_pe=pe_mm)
        self.cnt[eng] += 1
        me = (eng, self.cnt[eng])
        self.ops[eng].append((waits, fn, ("c", eng)))
        self._commit(me, reads, writes)
        return me

    def dma(self, eng, out, in_, reads=(), writes=(), cast=False, **kw):
        grp = "cast" if cast else eng
        lst = self.groups[grp]
        stream = lst[self.rr[grp]]
        self.rr[grp] = (self.rr[grp] + 1) % len(lst)
        key = ("d", stream)
        waits = self._deps(eng, reads, writes)
        prev = self.stream_cnt[stream] * 16
        if prev and self.waited[eng].get(key, 0) < prev:
            self.waited[eng][key] = prev
            waits.append((key, prev))
        self.stream_cnt[stream] += 1
        me = (key, self.stream_cnt[stream] * 16)
        self.ops[eng].append((waits, lambda e, o=out, i=in_, kw=kw: e.dma_start(out=o, in_=i, **kw), ("d", stream)))
        self._commit(me, reads, writes)
        return me

    def barrier(self, skip_cast=False):
        targets = [(e, self.cnt[e]) for e in self.ENG if self.cnt[e]]
        targets += [(("d", s), self.stream_cnt[s] * 16) for s in range(self.n_streams)
                    if self.stream_cnt[s] and not (skip_cast and s in self.groups["cast"])]
        for e in self.ENG:
            waits = []
            for k, v in targets:
                if self.waited[e].get(k, 0) < v:
                    self.waited[e][k] = v
                    waits.append((k, v))
            if waits:
                self.ops[e].append((waits, None, None))

    def act(self, out, in_, func, reads, writes, **kw):
        return self.op("act", lambda e: e.activation(out=out, in_=in_, func=func, **kw), reads, writes)

    def tt(self, eng, out, in0, in1, op, reads, writes):
        return self.op(eng, lambda e: e.tensor_tensor(out=out, in0=in0, in1=in1, op=op), reads, writes)

    def ts(self, eng, out, in0, s1, s2, op0, op1, reads, writes):
        if s2 is None:
            return self.op(eng, lambda e: e.tensor_scalar(out=out, in0=in0, scalar1=s1, scalar2=None, op0=op0), reads, writes)
        return self.op(eng, lambda e: e.tensor_scalar(out=out, in0=in0, scalar1=s1, scalar2=s2, op0=op0, op1=op1), reads, writes)

    def stt(self, eng, out, in0, scalar, in1, op0, op1, reads, writes):
        return self.op(eng, lambda e: e.scalar_tensor_tensor(out=out, in0=in0, scalar=scalar, in1=in1, op0=op0, op1=op1), reads, writes)

    def cp(self, eng, out, in_, reads, writes):
        if eng == "act":
            return self.op("act", lambda e: e.activation(out=out, in_=in_, func=AF.Copy), reads, writes)
        return self.op(eng, lambda e: e.tensor_copy(out=out, in_=in_), reads, writes)

    def memset(self, eng, ap, val, writes):
        return self.op(eng, lambda e: e.memset(ap, val), (), writes)

    def mm(self, out, lhsT, rhs, start, stop, reads, writes, tile_position=None):
        if tile_position is None:
            fn = lambda e: e.matmul(out, lhsT=lhsT, rhs=rhs, start=start, stop=stop)
        else:
            fn = lambda e: e.matmul(out, lhsT=lhsT, rhs=rhs, start=start, stop=stop, tile_position=tile_position)
        return self.op("pe", fn, reads, writes, pe_mm=True)

    def tr(self, out, in_, ident, reads, writes):
        return self.op("pe", lambda e: e.transpose(out=out, in_=in_, identity=ident), reads, writes, pe_mm=True)

    def emit(self):
        nc = self.nc
        with contextlib.ExitStack() as st:
            sems = {}
            for e in self.ENG:
                sems[e] = st.enter_context(nc.semaphore("s_" + e))
            for i in range(self.n_streams):
                sems[("d", i)] = st.enter_context(nc.semaphore("d%d" % i))
            block = st.enter_context(nc.Block())
            engmap = {"pe": "tensor", "act": "scalar", "dve": "vector", "pool": "gpsimd", "sp": "sync"}

            def make(ename):
                oplist = self.ops[ename]

                def body(eng):
                    for waits, fn, inc in oplist:
                        for k, v in waits:
                            eng.wait_ge(sems[k], v)
                        if fn is None:
                            continue
                        ins = fn(eng)
                        if inc[0] == "c":
                            ins.then_inc(sems[inc[1]], 1)
                        else:
                            ins.then_inc(sems[inc], 16)
                return body

            for e in self.ENG:
                if self.ops[e]:
                    getattr(block, engmap[e])(make(e))


class TB:
    def __init__(self, h):
        self.h = h
        self.b = Buf()

    def __getitem__(self, k):
        return self.h[k]


class Ring:
    def __init__(self, items):
        self.items = items
        self.i = 0

    def next(self):
        it = self.items[self.i]
        self.i = (self.i + 1) % len(self.items)
        return it


class _Stop(Exception):
    pass


CKPTS = []
SBUF_LEFT = []


def build_program(debug=False, limit=None):
    nc = bass.Bass("TRN2", target_bir_lowering=False)
    dr = lambda name, shape, dt=F32, kind="ExternalInput": nc.dram_tensor(name, list(shape), dt, kind=kind).ap()
    x = dr("x", [NSEQ, SEQ, D])
    meta_tokens = dr("meta_tokens", [NMETA, D])
    norm_mix_g = dr("norm_mix_g", [1, D])
    w_in = dr("w_in", [1, D, IN_W])
    b_forget = dr("b_forget", [1, H])
    w_attn_out = dr("w_attn_out", [1, 512, D])
    lam_re = dr("ssm_lambda_re", [1, NG, 64])
    lam_im = dr("ssm_lambda_im", [1, NG, 64])
    b_re = dr("ssm_b_re", [1, NG, 64, 16])
    b_im = dr("ssm_b_im", [1, NG, 64, 16])
    c_re = dr("ssm_c_re", [1, NG, 16, 64])
    c_im = dr("ssm_c_im", [1, NG, 16, 64])
    ssm_d = dr("ssm_d", [1, 512])
    log_dt = dr("ssm_log_dt", [1, NG])
    w_glu = dr("w_glu", [1, 512, 2 * D])
    w_o = dr("w_o", [1, D, D])
    norm_ffn_g = dr("norm_ffn_g", [1, D])
    w_up = dr("w_ffn_up", [1, D, 2 * DFF])
    conv_w = dr("ffn_conv_w", [1, 3, 2 * DFF])
    conv_b = dr("ffn_conv_b", [1, 2 * DFF])
    w_down = dr("w_ffn_down", [1, DFF, D])
    norm_final_g = dr("norm_final_g", [D])
    c_ident = dr("c_ident", [128, 128])
    c_mask = dr("c_mask", [128, 128])
    c_sel = dr("c_sel", [8, 8, 128])
    c_bd = dr("c_bd", [128, 128])
    y = dr("y", [NSEQ, SEQ, D], kind="ExternalOutput")
    ws_in = dr("ws_in", [4, 128, 8, 512], BF16, "Internal")
    ws_f = dr("ws_f", [128, 8, 8], BF16, "Internal")
    ws_gm = dr("ws_gm", [4, 128, 8, 512], BF16, "Internal")
    ws_zm = dr("ws_zm", [4, 128, 4, 512], BF16, "Internal")
    ws_am = dr("ws_am", [4, 128, 4, 256], BF16, "Internal")
    ws_o = dr("ws_o", [2, 128, 8, 512], BF16, "Internal")
    ws_up = dr("ws_up", [11, 128, 8, 512], BF16, "Internal")
    ws_dn = dr("ws_dn", [6, 128, 4, 1024], BF16, "Internal")

    st = contextlib.ExitStack()
    with st:
        def sb(name, shape, dt):
            return TB(st.enter_context(nc.sbuf_tensor(name, list(shape), dt)))

        def psb(name, shape, dt):
            t = TB(st.enter_context(nc.psum_tensor(name, list(shape), dt)))
            t.b.excl = True
            return t

        P = Prog(nc)

        def ckpt(name):
            CKPTS.append((name, dict(P.cnt)))
            if limit is not None and name == limit:
                if DBG.startswith('pad'):
                    eng_, n_ = DBG[3:].split(':')
                    for _ in range(int(n_)):
                        if eng_ == 'pe':
                            P.mm(pbanks[0].h[:, 0:128], identb[:, :], identb[:, :], True, True, [identb.b], [pbanks[0].b])
                        else:
                            P.memset(eng_, zt1[:], 0.0, [zt1.b])
                raise _Stop()

        identf = sb("identf", [128, 128], F32)
        identb = sb("identb", [128, 128], BF16)
        maskb = sb("maskb", [128, 128], BF16)
        selb = sb("selb", [128, 8, 128], BF16)
        bdm = sb("bdm", [128, 128], F32)
        ones = sb("ones", [128, 512], F32)
        gmix = sb("gmix", [128, 8], F32)
        gffn = sb("gffn", [128, 8], F32)
        gfin = sb("gfin", [128, D], F32)
        dcol = sb("dcol", [128, 4], F32)
        negb = sb("negb", [8, 1], F32)
        cwb = sb("cwb", [128, 44, 4], F32)
        wf = sb("wf", [128, 8, 8], BF16)
        KA = sb("KA", [128, 4, L], BF16)
        Vc = sb("Vc", [128, 17, 4, 192], BF16)
        FT = sb("FT", [128, 17, 8], F32)
        LagK = sb("LagK", [128, 4, TCH, 128], BF16)
        Gt = sb("Gt", [128, 4, TCH, 2, 128], BF16)
        Hst = sb("Hst", [128, TCH, 2, 512], BF16)
        CA = sb("CA", [128, 2, NSTRIP], F32)
        CB = sb("CB", [128, 2, NSTRIP], F32)
        xtok = sb("xtok", [128, 4, D], F32)
        xnb = [sb("xnb%d" % i, [128, D], BF16) for i in range(2)]
        ss = sb("ss", [128, 4], F32)
        rstd = sb("rstd", [128, 4], F32)
        xnT = sb("xnT", [128, 8, 512], BF16)
        xn2T = xnT
        QA = sb("QA", [128, 8, 512], BF16)
        QF = sb("QF", [128, 512], BF16)
        Fcar = sb("Fcar", [8, 1], F32)
        UT = sb("UT", [128, 4, 512], BF16)
        attnT = sb("attnT", [128, 4, 512], BF16)
        zT = sb("zT", [128, 4, 512], BF16)
        R8 = sb("R8", [128, 4096], BF16)
        R8lo = Buf()
        R8hi = Buf()
        R8b = [R8lo, R8hi]

        class View:
            def __init__(self, ap, b):
                self.ap = ap
                self.b = b

            def __getitem__(self, k):
                return self.ap[k]

        junk = View(R8.h[:, 0:1024], R8lo)
        mergedT = View(R8.h[:, :].rearrange("p (m n) -> p m n", n=512), None)
        Vsb = View(R8.h[:, :].bitcast(F32).rearrange("p (r s j) -> p r s j", r=2, s=NSTRIP), None)
        actT = [View(R8.h[:, 2048 * i:2048 * i + 2048].rearrange("p (m n) -> p m n", n=512), R8b[i]) for i in range(2)]
        PT = [sb("PT%d" % i, [128, 512], BF16) for i in range(3)]
        Sprev = sb("Sprev", [128, 2, NSTRIP, 64], BF16)
        Zs = [sb("Zs%d" % i, [128, 2, NSTRIP], F32) for i in range(2)]
        zt1 = sb("zt1", [128, 2, NSTRIP], F32)
        zt2 = sb("zt2", [128, 2, NSTRIP], F32)
        sig = [[sb("sig%d_%d" % (i, j), [128, 512], BF16) for j in range(3)] for i in range(2)]
        ft = [sb("ft%d" % i, [128, 512], F32) for i in range(5)]
        ysb, t1b, t2b = ft[0], ft[1], ft[2]
        cv = [ft[0], ft[1]]
        cg = [ft[2], ft[3]]
        sgb = ft[4]
        rec = [ft[3], ft[4]]
        fe = View(ft[0].h[0:8, :], ft[0].b)
        Fp = View(ft[1].h[0:8, :], ft[1].b)
        halo = sb("halo", [128, 44, 2], F32)
        halo_b = [Buf() for _ in range(44)]
        snapF = sb("snapF", [8, 1], F32)
        snapZ = sb("snapZ", [128, 2, NSTRIP], F32)
        snapH = sb("snapH", [128, 44, 2], F32)
        hb = [[sb("hb%d_%d" % (w_, i_), [128, 516], BF16) for i_ in range(2)] for w_ in range(2)]
        dg = [[[sb("dg%d_%d_%d" % (p_, w_, t_), [128, 128], BF16) for t_ in range(3)] for w_ in range(2)] for p_ in range(2)]
        NSLOT = 3
        slots = [sb("slot%d" % i, [128, 4096], BF16) for i in range(NSLOT)]
        slot_ring = Ring(slots)
        pbanks = [psb("pb%d" % i, [128, 512], F32) for i in range(6)]
        pring = Ring(pbanks)
        obanks = [psb("ob%d" % i, [128, 512], F32) for i in range(2)]
        oring = Ring(obanks)

        def bank_bf16(pb):
            return pb.h[:].bitcast(BF16)

        kp = lambda ap: ap.rearrange("(k p) j -> p k j", p=128)
        GB = 2056
        cast_bufs = {}

        def ensure_cast(name, c):
            key = (name, c)
            if key in cast_bufs:
                return cast_bufs[key]
            b = Buf()
            cast_bufs[key] = [b]
            cd = lambda o, i: P.dma("pool", o, i, writes=[b], cast=True)
            if name == "in":
                lo = [0, 512, 1024, 1544][c]
                cd(ws_in[c], kp(w_in[0][:, lo:lo + 512]))
            elif name == "f":
                cd(ws_f, kp(w_in[0][:, 1536:1544]))
            elif name == "gm":
                cd(ws_gm[c][:, :, 0:256], kp(w_in[0][:, GB + 256 * c: GB + 256 * c + 256]))
                cd(ws_gm[c][:, :, 256:512], kp(w_in[0][:, GB + 1024 + 256 * c: GB + 1024 + 256 * c + 256]))
            elif name == "zm":
                cd(ws_zm[c][:, :, 0:256], kp(w_glu[0][:, 256 * c: 256 * c + 256]))
                cd(ws_zm[c][:, :, 256:512], kp(w_glu[0][:, 1024 + 256 * c: 1024 + 256 * c + 256]))
            elif name == "am":
                cd(ws_am[c], kp(w_attn_out[0][:, 256 * c: 256 * c + 256]))
            elif name == "o":
                cd(ws_o[c], kp(w_o[0][:, 512 * c: 512 * c + 512]))
            elif name == "up":
                cd(ws_up[c][:, :, 0:256], kp(w_up[0][:, 256 * c: 256 * c + 256]))
                cd(ws_up[c][:, :, 256:512], kp(w_up[0][:, DFF + 256 * c: DFF + 256 * c + 256]))
            elif name == "dn":
                nt_ = 4 if c < 5 else 2
                cd(ws_dn[c][:, 0:nt_, :], w_down[0][512 * c: 512 * c + 128 * nt_, :].rearrange("(t p) j -> p t j", p=128))
            return cast_bufs[key]

        tmpb = Buf()

        def xt_tmp(i):
            return xtok.h[:, i // 2, (i % 2) * 512:(i % 2) * 512 + 512]

        def slot_tmp(i):
            return slots[i // 4].h[:, (i % 4) * 1024:(i % 4) * 1024 + 1024].bitcast(F32)

        class Tmp:
            def __init__(self, ap):
                self.ap = ap
                self.b = Buf()

            def g(self):
                return self.ap.rearrange("p (g c) -> p g c", c=16)

            def s(self):
                return self.ap[:, 0:32]

        def xn_tmp(i):
            return xnT.h[:, 2 * i:2 * i + 2, :].rearrange("p a b -> p (a b)").bitcast(F32)

        tmps = [Tmp(xt_tmp(i)) for i in range(8)] + [Tmp(slot_tmp(i)) for i in range(4 * NSLOT)] + [Tmp(xn_tmp(i)) for i in range(4)]
        free_tmps = list(tmps)

        def newtmp():
            return free_tmps.pop(0)

        def rel(*ts_):
            for t in ts_:
                free_tmps.append(t)

        try:
            P.dma("sp", identf[:], c_ident, writes=[identf.b])
            P.cp("dve", identb[:], identf[:], [identf.b], [identb.b])
            t_m = newtmp()
            P.dma("sp", t_m.ap[:, 0:128], c_mask, writes=[t_m.b])
            P.cp("dve", maskb[:], t_m.ap[:, 0:128], [t_m.b], [maskb.b])
            csel2 = c_sel.rearrange("r h k -> r (h k)")
            P.memset("dve", selb[:], 0.0, [selb.b])
            P.memset("dve", QA[:], 0.0, [QA.b])
            P.memset("dve", QF[:], 0.0, [QF.b])
            for hh in range(2):
                P.dma("sp", t_m.ap[0:8, :], csel2[:, 512 * hh:512 * hh + 512], reads=[], writes=[t_m.b])
                P.cp("dve", selb[0:8, 4 * hh:4 * hh + 4, :], t_m.ap[0:8, :].rearrange("p (h k) -> p h k", k=128), [t_m.b], [selb.b])
            P.dma("sp", bdm[:], c_bd, writes=[bdm.b])
            P.memset("pool", ones[:], 1.0, [ones.b])
            P.dma("sp", gfin[:], norm_final_g.partition_broadcast(128), writes=[gfin.b])
            P.dma("sp", wf[:], ws_f, reads=ensure_cast("f", 0), writes=[wf.b])
            P.memset("pool", Vc[:], 1.0, [Vc.b])
            P.memset("pool", FT[:], 0.0, [FT.b])

            for dst, src, r in ((gmix, norm_mix_g, 8), (gffn, norm_ffn_g, 8), (dcol, ssm_d, 4)):
                P.dma("sp", t_m.ap[0:r, 0:128], src[0].rearrange("(k p) -> k p", p=128), writes=[t_m.b])
                pb = pring.next()
                P.tr(pb.h[:, 0:r], t_m.ap[0:r, 0:128], identf[0:r, 0:r], [t_m.b, identf.b], [pb.b])
                P.cp("dve", dst[:], pb.h[:, 0:r], [pb.b], [dst.b])
            for part in range(11):
                P.dma("sp", t_m.ap[0:3, :], conv_w[0][:, part * 512:(part + 1) * 512], writes=[t_m.b])
                P.dma("sp", t_m.ap[3:4, :], conv_b[0][part * 512:(part + 1) * 512].rearrange("(o n) -> o n", o=1), writes=[t_m.b])
                pb = pring.next()
                for j in range(4):
                    P.tr(pb.h[:, 4 * j:4 * j + 4], t_m.ap[0:4, j * 128:(j + 1) * 128], identf[0:4, 0:4], [t_m.b, identf.b], [pb.b])
                P.cp("dve", cwb[:, 4 * part:4 * part + 4, :], pb.h[:, 0:16].rearrange("p (i j) -> p i j", j=4), [pb.b], [cwb.b])
            P.dma("sp", negb[:], b_forget[0].rearrange("(h o) -> h o", o=1), writes=[negb.b])
            P.ts("dve", negb[:], negb[:], -1.0, None, ALU.mult, None, [negb.b], [negb.b])

            def dup_T(src):
                P.dma("sp", t_m.ap[0:32, 0:64], src, writes=[t_m.b])
                P.dma("sp", t_m.ap[0:32, 64:128], src, writes=[t_m.b])
                pb = pring.next()
                P.tr(pb.h[:, 0:32], t_m.ap[0:32, 0:128], identf[0:32, 0:32], [t_m.b, identf.b], [pb.b])
                o = newtmp()
                P.cp("dve", o.s(), pb.h[:, 0:32], [pb.b], [o.b])
                return o

            LR = dup_T(lam_re[0])
            LI = dup_T(lam_im[0])
            DT = newtmp()
            P.dma("sp", DT.s(), log_dt[0].partition_broadcast(128), writes=[DT.b])
            P.act(DT.s(), DT.s(), AF.Exp, [DT.b], [DT.b])
            PM = newtmp()
            P.memset("pool", PM.s(), 0.0, [PM.b])
            P.memset("pool", PM.ap[0:64, 0:32].rearrange("p (s two) -> p s two", two=2)[:, :, 0], 1.0, [PM.b])
            P.memset("pool", PM.ap[64:128, 0:32].rearrange("p (s two) -> p s two", two=2)[:, :, 1], 1.0, [PM.b])
            for _c in range(4):
                ensure_cast("in", _c)
            for _c in range(4):
                ensure_cast("gm", _c)
                ensure_cast("zm", _c)
                ensure_cast("am", _c)
            for _c in range(2):
                ensure_cast("o", _c)
            for _g in range(6):
                ensure_cast("up", 2 * _g)
                if _g < 5:
                    ensure_cast("up", 2 * _g + 1)
                ensure_cast("dn", _g)

            TWO_PI = 2.0 * math.pi

            def sin_of(dst, src, shift, scratch_f, scratch_i):
                a = dst.s()
                kf = scratch_f.s()
                ki = scratch_i.ap[:, 0:32].bitcast(I32)
                P.ts("dve", a, src.s(), 1.0, shift, ALU.mult, ALU.add, [src.b], [dst.b])
                P.ts("dve", kf, a, 1.0 / TWO_PI, None, ALU.mult, None, [dst.b], [scratch_f.b])
                P.cp("dve", ki, kf, [scratch_f.b], [scratch_i.b])
                P.cp("dve", kf, ki, [scratch_i.b], [scratch_f.b])
                P.stt("dve", a, kf, -TWO_PI, a, ALU.mult, ALU.add, [scratch_f.b, dst.b], [dst.b])
                P.ts("dve", kf, a, -math.pi, None, ALU.is_lt, None, [dst.b], [scratch_f.b])
                P.stt("dve", a, kf, TWO_PI, a, ALU.mult, ALU.add, [scratch_f.b, dst.b], [dst.b])
                P.ts("dve", kf, a, math.pi, None, ALU.is_gt, None, [dst.b], [scratch_f.b])
                P.stt("dve", a, kf, -TWO_PI, a, ALU.mult, ALU.add, [scratch_f.b, dst.b], [dst.b])
                P.ts("dve", a, a, -3.141592, 3.141592, ALU.max, ALU.min, [dst.b], [dst.b])
                P.act(a, a, AF.Sin, [dst.b], [dst.b])

            AR = newtmp()
            AI = newtmp()
            P.tt("dve", AR.s(), LR.s(), DT.s(), ALU.mult, [LR.b, DT.b], [AR.b])
            P.tt("dve", AI.s(), LI.s(), DT.s(), ALU.mult, [LI.b, DT.b], [AI.b])
            MAG = newtmp()
            P.act(MAG.s(), AR.s(), AF.Exp, [AR.b], [MAG.b])
            SN = newtmp()
            CS = newtmp()
            SC1 = newtmp()
            SC2 = newtmp()
            sin_of(SN, AI, 0.0, SC1, SC2)
            sin_of(CS, AI, math.pi / 2, SC1, SC2)
            LBr = AR
            LBi = AI
            P.tt("dve", LBr.s(), MAG.s(), CS.s(), ALU.mult, [MAG.b, CS.b], [LBr.b])
            P.tt("dve", LBi.s(), MAG.s(), SN.s(), ALU.mult, [MAG.b, SN.b], [LBi.b])
            NR = MAG
            P.ts("dve", NR.s(), LBr.s(), -1.0, None, ALU.add, None, [LBr.b], [NR.b])
            DEN = SN
            P.tt("dve", DEN.s(), LR.s(), LR.s(), ALU.mult, [LR.b], [DEN.b])
            P.tt("dve", SC1.s(), LI.s(), LI.s(), ALU.mult, [LI.b], [SC1.b])
            P.tt("dve", DEN.s(), DEN.s(), SC1.s(), ALU.add, [DEN.b, SC1.b], [DEN.b])
            P.op("dve", lambda e: e.reciprocal(out=DEN.s(), in_=DEN.s()), [DEN.b], [DEN.b])
            KR = CS
            KI = SC2
            P.tt("dve", KR.s(), NR.s(), LR.s(), ALU.mult, [NR.b, LR.b], [KR.b])
            P.tt("dve", SC1.s(), LBi.s(), LI.s(), ALU.mult, [LBi.b, LI.b], [SC1.b])
            P.tt("dve", KR.s(), KR.s(), SC1.s(), ALU.add, [KR.b, SC1.b], [KR.b])
            P.tt("dve", KR.s(), KR.s(), DEN.s(), ALU.mult, [KR.b, DEN.b], [KR.b])
            P.tt("dve", KI.s(), LBi.s(), LR.s(), ALU.mult, [LBi.b, LR.b], [KI.b])
            P.tt("dve", SC1.s(), NR.s(), LI.s(), ALU.mult, [NR.b, LI.b], [SC1.b])
            P.tt("dve", KI.s(), KI.s(), SC1.s(), ALU.subtract, [KI.b, SC1.b], [KI.b])
            P.tt("dve", KI.s(), KI.s(), DEN.s(), ALU.mult, [KI.b, DEN.b], [KI.b])
            P.tt("dve", KR.s(), KR.s(), PM.s(), ALU.mult, [KR.b, PM.b], [KR.b])
            P.tt("dve", KI.s(), KI.s(), PM.s(), ALU.mult, [KI.b, PM.b], [KI.b])
            rel(LR, LI, DT, MAG, SN, SC1)

            def bc(t):
                return t.s().unsqueeze(2).to_broadcast([128, 32, 16])

            Br = newtmp()
            Bi = newtmp()
            for dst, src in ((Br, b_re), (Bi, b_im)):
                for half in range(2):
                    for q4 in range(4):
                        P.dma("sp", dst.g()[64 * half:64 * half + 64, 8 * q4:8 * q4 + 8, :],
                              src[0][8 * q4:8 * q4 + 8].rearrange("g p c -> p g c"), writes=[dst.b])
            Er = newtmp()
            Ei = newtmp()
            W1 = newtmp()
            W2 = newtmp()
            P.tt("dve", Er.g(), Br.g(), bc(KR), ALU.mult, [Br.b, KR.b], [Er.b])
            P.tt("dve", W1.g(), Bi.g(), bc(KI), ALU.mult, [Bi.b, KI.b], [W1.b])
            P.tt("dve", Er.g(), Er.g(), W1.g(), ALU.subtract, [Er.b, W1.b], [Er.b])
            P.tt("dve", Ei.g(), Bi.g(), bc(KR), ALU.mult, [Bi.b, KR.b], [Ei.b])
            P.tt("dve", W1.g(), Br.g(), bc(KI), ALU.mult, [Br.b, KI.b], [W1.b])
            P.tt("dve", Ei.g(), Ei.g(), W1.g(), ALU.add, [Ei.b, W1.b], [Ei.b])
            rel(KR, KI)
            Fr = Br
            Fi = Bi
            tn = newtmp()
            for dst, src in ((Fr, c_re), (Fi, c_im)):
                for ct in range(4):
                    srcv = src[0][8 * ct:8 * ct + 8].rearrange("g c p -> (g c) p")
                    P.dma("sp", tn.ap[:, ct * 128:ct * 128 + 64], srcv, writes=[tn.b])
                    P.dma("sp", tn.ap[:, ct * 128 + 64:ct * 128 + 128], srcv, writes=[tn.b])
                pb = pring.next()
                for ct in range(4):
                    P.tr(pb.h[:, ct * 128:(ct + 1) * 128], tn.ap[:, ct * 128:(ct + 1) * 128], identf[:], [tn.b, identf.b], [pb.b])
                P.tt("dve", dst.g(), pb.h[:, :].rearrange("p (g c) -> p g c", c=16), bc(PM), ALU.mult, [pb.b, PM.b], [dst.b])
            rel(tn, PM)
            nFi0 = newtmp()
            P.ts("dve", nFi0.ap, Fi.ap, -1.0, None, ALU.mult, None, [Fi.b], [nFi0.b])
            Fr0 = newtmp()
            P.cp("dve", Fr0.ap, Fr.ap, [Fr.b], [Fr0.b])

            def cmul_step(Xr, Xi, Wa, Wb):
                P.tt("dve", Wa.g(), Xr.g(), bc(LBr), ALU.mult, [Xr.b, LBr.b], [Wa.b])
                P.tt("dve", Wb.g(), Xi.g(), bc(LBi), ALU.mult, [Xi.b, LBi.b], [Wb.b])
                P.tt("dve", Wa.g(), Wa.g(), Wb.g(), ALU.subtract, [Wa.b, Wb.b], [Wa.b])
                P.tt("dve", Wb.g(), Xr.g(), bc(LBi), ALU.mult, [Xr.b, LBi.b], [Wb.b])
                P.tt("dve", Xi.g(), Xi.g(), bc(LBr), ALU.mult, [Xi.b, LBr.b], [Xi.b])
                P.tt("dve", Xi.g(), Xi.g(), Wb.g(), ALU.add, [Xi.b, Wb.b], [Xi.b])
                P.cp("dve", Xr.ap, Wa.ap, [Wa.b], [Xr.b])

            for n in range(TCH):
                for ri, E in ((0, Er), (1, Ei)):
                    pb = pring.next()
                    for ct in range(4):
                        P.tr(pb.h[:, ct * 128:(ct + 1) * 128], E.ap[:, ct * 128:(ct + 1) * 128], identf[:], [E.b, identf.b], [pb.b])
                    P.cp("act", Gt[:, :, n, ri, :], pb.h[:, :].rearrange("p (t c) -> p t c", c=128), [pb.b], [Gt.b])
                pb = pring.next()
                for ct in range(4):
                    cs_ = slice(ct * 128, (ct + 1) * 128)
                    P.mm(pb.h[:, cs_], Er.ap[:, cs_], Fr0.ap[:, cs_], True, False, [Er.b, Fr0.b], [pb.b])
                    P.mm(pb.h[:, cs_], Ei.ap[:, cs_], nFi0.ap[:, cs_], False, True, [Ei.b, nFi0.b], [pb.b])
                for ct in range(4):
                    cs_ = slice(ct * 128, (ct + 1) * 128)
                    if n == 0:
                        P.tt("dve", W1.ap[:, 0:128], pb.h[:, cs_], bdm[:], ALU.mult, [pb.b, bdm.b], [W1.b])
                        P.stt("dve", LagK[:, ct, 0, :], identf[:], dcol[:, ct:ct + 1], W1.ap[:, 0:128], ALU.mult, ALU.add,
                              [identf.b, dcol.b, W1.b], [LagK.b])
                    else:
                        P.tt("dve", LagK[:, ct, n, :], pb.h[:, cs_], bdm[:], ALU.mult, [pb.b, bdm.b], [LagK.b])
                if n < TCH - 1:
                    cmul_step(Er, Ei, W1, W2)
            for n in range(1, TCH + 1):
                cmul_step(Fr, Fi, W1, W2)
                P.cp("act", Hst[:, n - 1, 0, :], Fr.ap, [Fr.b], [Hst.b])
                P.ts("dve", Hst[:, n - 1, 1, :], Fi.ap, -1.0, None, ALU.mult, None, [Fi.b], [Hst.b])
            Pr = newtmp()
            Pi = newtmp()
            P.cp("dve", Pr.s(), LBr.s(), [LBr.b], [Pr.b])
            P.cp("dve", Pi.s(), LBi.s(), [LBi.b], [Pi.b])
            for _ in range(3):
                P.tt("dve", W1.s(), Pr.s(), Pr.s(), ALU.mult, [Pr.b], [W1.b])
                P.tt("dve", W2.s(), Pi.s(), Pi.s(), ALU.mult, [Pi.b], [W2.b])
                P.tt("dve", W1.s(), W1.s(), W2.s(), ALU.subtract, [W1.b, W2.b], [W1.b])
                P.tt("dve", W2.s(), Pr.s(), Pi.s(), ALU.mult, [Pr.b, Pi.b], [W2.b])
                P.ts("dve", Pi.s(), W2.s(), 2.0, None, ALU.mult, None, [W2.b], [Pi.b])
                P.cp("dve", Pr.s(), W1.s(), [W1.b], [Pr.b])
            for half in range(2):
                hs = slice(64 * half, 64 * half + 64)
                srcr = Pr.ap[hs, 0:32].rearrange("p (s two) -> p s two", two=2)[:, :, half]
                srci = Pi.ap[hs, 0:32].rearrange("p (s two) -> p s two", two=2)[:, :, half]
                P.cp("dve", CA[hs, 0, :], srcr, [Pr.b], [CA.b])
                P.cp("dve", CA[hs, 1, :], srcr, [Pr.b], [CA.b])
                P.ts("dve", CB[hs, 0, :], srci, -1.0, None, ALU.mult, None, [Pi.b], [CB.b])
                P.cp("dve", CB[hs, 1, :], srci, [Pi.b], [CB.b])

            ckpt("phase1")
            P.barrier()

            v8 = lambda s: s.h[:, :].rearrange("p (k j) -> p k j", j=512)
            v4 = lambda s: s.h[:, 0:2048].rearrange("p (k j) -> p k j", j=512)
            vam = lambda s: s.h[:, 2048:3072].rearrange("p (k j) -> p k j", j=256)
            vdn = lambda s: s.h[:, :].rearrange("p (t j) -> p t j", j=1024)

            WDEPTH = NSLOT - 2

            def block_reqs(is_meta):
                r = [[("in", c, ws_in[c], v8)] for c in range(4)]
                for m2 in range(4):
                    r.append([("gm", m2, ws_gm[m2], v8)])
                    r.append([("zm", m2, ws_zm[m2], v4), ("am", m2, ws_am[m2], vam)])
                r += [[("o", hh, ws_o[hh], v8)] for hh in range(2)]
                for g in range(6):
                    nunits = 2 if g < 5 else 1
                    def dnreq(gd):
                        nud = 2 if gd < 5 else 1
                        return [("dn", gd, ws_dn[gd][:, 0:2 * nud, :], (lambda s, nu=nud: vdn(s)[:, 0:2 * nu, :]))]
                    for uu in range(nunits):
                        r.append([("up", 2 * g + uu, ws_up[2 * g + uu], v8)])
                        if not is_meta and uu == 0 and g >= 1:
                            r.append(dnreq(g - 1))
                    if not is_meta and g == 5:
                        r.append(dnreq(5))
                return r

            all_reqs = []
            for _seq in range(NSEQ):
                for _m in ((True, False, False, False, False) if _seq == 0 else (False, False, False, False)):
                    all_reqs += block_reqs(_m)
            wq_state = [0, 0]

            def wnext(expect):
                i = wq_state[0]
                wq_state[0] += 1
                assert all_reqs[i][0][0] == expect[0] and all_reqs[i][0][1] == expect[1], (all_reqs[i][0][:2], expect)
                while wq_state[1] < len(all_reqs) and wq_state[1] <= i + WDEPTH:
                    j = wq_state[1]
                    slot = slots[j % NSLOT]
                    for (name, c, src, vf) in all_reqs[j]:
                        P.dma("sp", vf(slot), src, reads=ensure_cast(name, c), writes=[slot.b])
                    wq_state[1] += 1
                slot = slots[i % NSLOT]
                views = [vf(slot) for (_n, _c, _s, vf) in all_reqs[i]]
                return (slot, views[0]) if len(views) == 1 else (slot, views)

            def rmsnorm_T(src_tb, n, tp, nt, gcol, dstT):
                P.memset(cur_eng[0], ss[:], 0.0, [ss.b])
                for t in range(nt):
                    P.act(junk[0:tp, :], src_tb[0:tp, t, :], AF.Square, [src_tb.b, ss.b], [junk.b, ss.b], accum_out=ss[0:tp, t:t + 1])
                P.act(rstd[0:tp, 0:nt], ss[0:tp, 0:nt], AF.Sqrt, [ss.b], [rstd.b], scale=1.0 / D, bias=EPS)
                P.op("dve", lambda e: e.reciprocal(out=rstd[0:tp, 0:nt], in_=rstd[0:tp, 0:nt]), [rstd.b], [rstd.b])
                for t in range(nt):
                    xb_ = xnb[t % 2]
                    P.act(xb_[0:tp, :], src_tb[0:tp, t, :], AF.Identity, [src_tb.b, rstd.b], [xb_.b], scale=rstd[0:tp, t:t + 1])
                    pb = pring.next()
                    pv_ = bank_bf16(pb).rearrange("p (k c) -> p k c", c=128)
                    for k in range(KT):
                        P.tr(pv_[:, k, 0:tp], xb_[0:tp, k * 128:(k + 1) * 128], identb[0:tp, 0:tp], [xb_.b, identb.b], [pb.b])
                    P.tt("dve", dstT[:, :, t * 128:t * 128 + tp], pv_[:, :, 0:tp],
                         gcol[:, :].unsqueeze(2).to_broadcast([128, KT, tp]), ALU.mult, [pb.b, gcol.b], [dstT.b])

            zcur = [0]
            cur_eng = ["pool"]

            def scan_steps(j0, j1):
                for j in range(j0, j1):
                    Zc = Zs[zcur[0]]
                    Zn = Zs[1 - zcur[0]]
                    P.cp(cur_eng[0], Sprev[:, :, :, j], Zc[:], [Zc.b], [Sprev.b])
                    P.tt("dve", zt1[:], CA[:], Zc[:], ALU.mult, [CA.b, Zc.b], [zt1.b])
                    P.tt("dve", zt2[:, 0, :], CB[:, 0, :], Zc[:, 1, :], ALU.mult, [CB.b, Zc.b], [zt2.b])
                    P.tt("dve", zt2[:, 1, :], CB[:, 1, :], Zc[:, 0, :], ALU.mult, [CB.b, Zc.b], [zt2.b])
                    P.tt("dve", zt1[:], zt1[:], zt2[:], ALU.add, [zt1.b, zt2.b], [zt1.b])
                    P.tt("dve", Zn[:], zt1[:], Vsb[:, :, :, j], ALU.add, [zt1.b] + R8b, [Zn.b])
                    zcur[0] = 1 - zcur[0]

            for seq in range(NSEQ):
                xblocks = [(NMETA + 512 * i, 512, False, i) for i in range(4)]
                if seq == 0:
                    P.memset("dve", Fcar[:], 0.0, [Fcar.b])
                    P.memset("dve", Zs[zcur[0]][:], 0.0, [Zs[zcur[0]].b])
                    P.memset("dve", halo[:], 0.0, halo_b)
                    blocks = [(0, NMETA, True, 0)] + xblocks
                else:
                    P.cp("pool", Fcar[:], snapF[:], [snapF.b], [Fcar.b])
                    P.cp("pool", Zs[zcur[0]][:], snapZ[:], [snapZ.b], [Zs[zcur[0]].b])
                    P.cp("pool", halo[:], snapH[:], [snapH.b], halo_b)
                    blocks = xblocks
                for (pos0, n, is_meta, bi) in blocks:
                    cur_eng[0] = "dve" if is_meta else "pool"
                    tp = min(n, 128)
                    nt = n // tp
                    nch = n // TCH
                    if is_meta:
                        P.dma("sp", xtok[0:NMETA, 0, :], meta_tokens, writes=[xtok.b])
                    else:
                        P.dma("pool", xtok[:, :, :], x[seq, bi * 512:(bi + 1) * 512, :].rearrange("(t p) d -> p t d", p=128), writes=[xtok.b])
                    rmsnorm_T(xtok, n, tp, nt, gmix, xnT)
                    ckpt("s1_%d_%d" % (seq, pos0))
                    sq, wq = wnext(("in", 0))
                    sk, wk = wnext(("in", 1))
                    for j in range(4):
                        pb = pring.next()
                        for k in range(KT):
                            P.mm(pb.h[:, 0:n], wq[:, k, j * 128:(j + 1) * 128], xnT[:, k, 0:n], k == 0, k == KT - 1, [sq.b, xnT.b], [pb.b])
                        for half in range(2):
                            rows = slice(64 * half, 64 * half + 64)
                            P.act(QA[rows, 2 * j + half, 0:n], pb.h[rows, 0:n], AF.Copy, [pb.b], [QA.b], scale=0.125)
                    for j in range(4):
                        pb = pring.next()
                        for k in range(KT):
                            P.mm(pb.h[:, 0:n], wk[:, k, j * 128:(j + 1) * 128], xnT[:, k, 0:n], k == 0, k == KT - 1, [sk.b, xnT.b], [pb.b])
                        P.cp("dve", KA[:, j, pos0:pos0 + n], pb.h[:, 0:n], [pb.b], [KA.b])
                    pb = pring.next()
                    for k in range(KT):
                        P.mm(pb.h[0:8, 0:n], wf[:, k, :], xnT[:, k, 0:n], k == 0, k == KT - 1, [wf.b, xnT.b], [pb.b])
                    P.act(fe[:, 0:n], pb.h[0:8, 0:n], AF.Exp, [pb.b, negb.b], [fe.b], scale=-1.0, bias=negb[:, 0:1])
                    P.act(fe[:, 0:n], fe[:, 0:n], AF.Ln, [fe.b], [fe.b], bias=1.0)
                    P.op("dve", lambda e, n=n: e.tensor_tensor_scan(out=Fp[:, 0:n], data0=ones[0:8, 0:n], data1=fe[:, 0:n],
                                                                  initial=Fcar[:, 0:1], op0=ALU.mult, op1=ALU.add),
                         [ones.b, fe.b, Fcar.b], [Fp.b])
                    P.cp("dve", Fcar[:, 0:1], Fp[:, n - 1:n], [Fp.b], [Fcar.b])
                    P.ts("dve", QF[0:8, 0:n], Fp[:, 0:n], -1.0, None, ALU.mult, None, [Fp.b], [QF.b])
                    pb = pring.next()
                    for t in range(nt):
                        P.tr(pb.h[0:tp, 8 * t:8 * t + 8], Fp[:, t * 128:t * 128 + tp], identf[0:8, 0:8], [Fp.b, identf.b], [pb.b])
                    kt0 = 0 if is_meta else 1 + 4 * bi
                    P.cp("dve", FT[0:tp, kt0:kt0 + nt, :], pb.h[0:tp, 0:8 * nt].rearrange("p (t h) -> p t h", h=8), [pb.b], [FT.b])
                    sv, wv = wnext(("in", 2))
                    for t in range(nt):
                        pb = pring.next()
                        for k in range(KT):
                            P.mm(pb.h[0:tp, :], xnT[:, k, t * 128:t * 128 + tp], wv[:, k, :], k == 0, k == KT - 1, [sv.b, xnT.b], [pb.b])
                        dstv = Vc[0:tp, kt0 + t, :, :].rearrange("p j (a c) -> p j a c", c=64)[:, :, 0:3:2, :]
                        P.cp("dve", dstv, pb.h[0:tp, :].rearrange("p (j a c) -> p j a c", a=2, c=64), [pb.b], [Vc.b])
                    su, wu = wnext(("in", 3))
                    for ct in range(4):
                        pb = pring.next()
                        for k in range(KT):
                            P.mm(pb.h[:, 0:n], wu[:, k, ct * 128:(ct + 1) * 128], xnT[:, k, 0:n], k == 0, k == KT - 1, [su.b, xnT.b], [pb.b])
                        P.cp("act", UT[:, ct, 0:n], pb.h[:, 0:n], [pb.b], [UT.b])
                    ckpt("s2_%d_%d" % (seq, pos0))
                    vb = [pring.next() for _ in range(4)]
                    vq = [b_.h[:, :].rearrange("p (r c j) -> p r c j", r=2, c=4) for b_ in vb]
                    for ct in range(4):
                        utv = UT[:, ct, 0:n].rearrange("p (j i) -> p j i", i=TCH)
                        for ri in range(2):
                            for ip in range(TCH):
                                for sl in range(4):
                                    rs_ = slice(32 * sl, 32 * sl + 32)
                                    P.mm(vq[sl][:, ri, ct, 0:nch], Gt[rs_, ct, TCH - 1 - ip, ri, :], utv[rs_, :, ip],
                                         ip == 0, ip == TCH - 1, [Gt.b, UT.b], [vb[sl].b], tile_position=(32 * sl, 0))
                    for sl in range(4):
                        dst = Vsb.ap.rearrange("p r (c s) j -> p r c s j", s=4)[:, :, :, sl, 0:nch]
                        P.cp("act", dst, vq[sl][:, :, :, 0:nch], [vb[sl].b], R8b)
                    ckpt("s3_%d_%d" % (seq, pos0))
                    if is_meta:
                        ktiles = [(0, NMETA, True, 0)]
                    else:
                        ktiles = [(0, NMETA, False, 0)] + [(1 + t, 128, False, 0) for t in range(4 * bi)] + \
                                 [(1 + 4 * bi + jj, 128, True, 128 * jj) for jj in range(4)]
                    steps_per_head = (nch + H - 1) // H
                    sdone = [0]
                    obs = {}
                    ptc = [0]

                    def emit_S(h, ti):
                        j, half = h // 2, h % 2
                        rows = slice(64 * half, 64 * half + 64)
                        kt, nk, diag, c0 = ktiles[ti]
                        kp0 = 0 if kt == 0 else NMETA + 128 * (kt - 1)
                        sbk = pring.next()
                        P.mm(sbk.h[0:nk, c0:n], KA[:, j, kp0:kp0 + nk], QA[:, h, c0:n], True, False, [KA.b, QA.b], [sbk.b])
                        P.mm(sbk.h[0:nk, c0:n], selb[:, h, 0:nk], QF[:, c0:n], False, not diag, [selb.b, QF.b], [sbk.b])
                        if diag:
                            P.mm(sbk.h[0:nk, c0:c0 + nk], identb[0:nk, 0:nk], maskb[0:nk, 0:nk], False, True, [identb.b, maskb.b], [sbk.b])
                        pt = PT[ptc[0] % 3]
                        ptc[0] += 1
                        P.act(pt[0:nk, c0:n], sbk.h[0:nk, c0:n], AF.Exp, [sbk.b, FT.b], [pt.b], bias=FT[0:nk, kt, h:h + 1], scale=1.0)
                        return pt

                    def emit_PV(h, ti, pt):
                        j, half = h // 2, h % 2
                        rows = slice(64 * half, 64 * half + 64)
                        kt, nk, diag, c0 = ktiles[ti]
                        if ti == 0:
                            obs[h] = oring.next()
                        ob = obs[h]
                        P.mm(ob.h[:, c0:n], Vc[0:nk, kt, j, 64 * half:64 * half + 128], pt[0:nk, c0:n], ti == 0, ti == len(ktiles) - 1,
                             [Vc.b, pt.b], [ob.b])
                        if ti == len(ktiles) - 1:
                            rc = rec[h % 2]
                            srows = slice(64, 128) if half == 0 else slice(0, 64)
                            P.op("dve", lambda e, rc=rc, ob=ob, rows=rows, srows=srows, n=n: e.reciprocal(out=rc[rows, 0:n], in_=ob.h[srows, 0:n]),
                                 [ob.b], [rc.b])
                            P.tt("dve", attnT[rows, j, 0:n], ob.h[rows, 0:n], rc[rows, 0:n], ALU.mult, [ob.b, rc.b], [attnT.b])
                            s1 = min(nch, sdone[0] + steps_per_head)
                            scan_steps(sdone[0], s1)
                            sdone[0] = s1

                    pend = []
                    for h in range(H):
                        for ti in range(len(ktiles)):
                            pt = emit_S(h, ti)
                            pend.append((h, ti, pt))
                            if len(pend) > 2:
                                emit_PV(*pend.pop(0))
                    while pend:
                        emit_PV(*pend.pop(0))
                    scan_steps(sdone[0], nch)
                    ckpt("s4_%d_%d" % (seq, pos0))
                    for ct in range(4):
                        pb = pring.next()
                        Yv = pb.h[:, 0:n].rearrange("p (j i) -> p j i", i=TCH)
                        utv = UT[:, ct, 0:n].rearrange("p (j i) -> p j i", i=TCH)
                        for tau in range(TCH):
                            P.mm(Yv[:, :, tau:TCH], LagK[:, ct, tau, :], utv[:, :, 0:TCH - tau], tau == 0, False, [LagK.b, UT.b], [pb.b])
                        for i in range(TCH):
                            for sl in range(4):
                                for ri in range(2):
                                    last = (i == TCH - 1 and sl == 3 and ri == 1)
                                    P.mm(Yv[32 * sl:32 * sl + 32, :, i], Hst[:, i, ri, 128 * ct + 32 * sl:128 * ct + 32 * sl + 32],
                                         Sprev[:, ri, 4 * ct + sl, 0:nch], False, last, [Hst.b, Sprev.b], [pb.b], tile_position=(0, 32 * sl))
                        P.act(zT[:, ct, 0:n], pb.h[:, 0:n], AF.Gelu, [pb.b], [zT.b])
                    ckpt("s5_%d_%d" % (seq, pos0))
                    for m2 in range(4):
                        sg_, wg = wnext(("gm", m2))
                        sz_, (wz, wa) = wnext(("zm", m2))
                        for mm_ in range(2):
                            m = 2 * m2 + mm_
                            sset = sig[m % 2]
                            c1 = slice(mm_ * 128, mm_ * 128 + 128)
                            c2 = slice(256 + mm_ * 128, 256 + mm_ * 128 + 128)
                            for which, cs_ in ((0, c1), (1, c2)):
                                pb = pring.next()
                                for k in range(KT):
                                    P.mm(pb.h[:, 0:n], wg[:, k, cs_], xnT[:, k, 0:n], k == 0, k == KT - 1, [sg_.b, xnT.b], [pb.b])
                                P.act(sset[which][:, 0:n], pb.h[:, 0:n], AF.Sigmoid, [pb.b], [sset[which].b])
                            pb = pring.next()
                            for k in range(4):
                                P.mm(pb.h[:, 0:n], wz[:, k, c2], zT[:, k, 0:n], k == 0, k == 3, [sz_.b, zT.b], [pb.b])
                            P.act(sset[2][:, 0:n], pb.h[:, 0:n], AF.Sigmoid, [pb.b], [sset[2].b])
                            pb = pring.next()
                            for k in range(4):
                                P.mm(pb.h[:, 0:n], wz[:, k, c1], zT[:, k, 0:n], k == 0, k == 3, [sz_.b, zT.b], [pb.b])
                            P.tt("dve", ysb[:, 0:n], pb.h[:, 0:n], sset[2][:, 0:n], ALU.mult, [pb.b, sset[2].b], [ysb.b])
                            P.tt(cur_eng[0], t2b[:, 0:n], ysb[:, 0:n], sset[1][:, 0:n], ALU.mult, [ysb.b, sset[1].b], [t2b.b])
                            pb = pring.next()
                            for k in range(4):
                                P.mm(pb.h[:, 0:n], wa[:, k, c1], attnT[:, k, 0:n], k == 0, k == 3, [sz_.b, attnT.b], [pb.b])
                            P.tt("dve", t1b[:, 0:n], pb.h[:, 0:n], sset[0][:, 0:n], ALU.mult, [pb.b, sset[0].b], [t1b.b])
                            P.tt(cur_eng[0], mergedT[:, m, 0:n], t1b[:, 0:n], t2b[:, 0:n], ALU.add, [t1b.b, t2b.b], R8b)
                    for hh in range(2):
                        so, wo = wnext(("o", hh))
                        for t in range(nt):
                            pb = pring.next()
                            for k in range(KT):
                                P.mm(pb.h[0:tp, :], mergedT[:, k, t * 128:t * 128 + tp], wo[:, k, :], k == 0, k == KT - 1, [so.b] + R8b, [pb.b])
                            dsth = xtok[0:tp, t, hh * 512:(hh + 1) * 512]
                            P.tt("dve", dsth, dsth, pb.h[0:tp, :], ALU.add, [pb.b, xtok.b], [xtok.b])
                    ckpt("s6_%d_%d" % (seq, pos0))
                    rmsnorm_T(xtok, n, tp, nt, gffn, xn2T)

                    def ffn_down(g, n=n, tp=tp, nt=nt):
                        nu = 2 if g < 5 else 1
                        at = actT[g % 2]
                        sd, wd = wnext(("dn", g))
                        for t in range(nt):
                            for hh in range(2):
                                pb = pring.next()
                                for il in range(2 * nu):
                                    P.mm(pb.h[0:tp, :], at[:, il, t * 128:t * 128 + tp], wd[:, il, hh * 512:(hh + 1) * 512],
                                         il == 0, il == 2 * nu - 1, [at.b, sd.b], [pb.b])
                                dsth = xtok[0:tp, t, hh * 512:(hh + 1) * 512]
                                P.tt("dve", dsth, dsth, pb.h[0:tp, :], ALU.add, [pb.b, xtok.b], [xtok.b])

                    for g in range(6):
                        nunits = 2 if g < 5 else 1
                        at = actT[g % 2]
                        for uu in range(nunits):
                            c = 2 * g + uu
                            sup, wup = wnext(("up", c))
                            for e_ in range(2):
                                i = 2 * c + e_
                                il = 2 * uu + e_
                                parts = ((0, slice(e_ * 128, e_ * 128 + 128), i, cv[i % 2]),
                                         (1, slice(256 + e_ * 128, 256 + e_ * 128 + 128), NFT + i, cg[i % 2]))
                                for which, cs_, ci, dst in parts:
                                    pb = pring.next()
                                    for k in range(KT):
                                        P.mm(pb.h[:, 0:n], wup[:, k, cs_], xn2T[:, k, 0:n], k == 0, k == KT - 1, [sup.b, xn2T.b], [pb.b])
                                    hbuf = hb[which][i % 2]
                                    P.cp(cur_eng[0], hbuf[:, 0:2], halo[:, ci, :], [halo_b[ci]], [hbuf.b])
                                    P.cp("dve", hbuf[:, 2:2 + n], pb.h[:, 0:n], [pb.b], [hbuf.b])
                                    P.cp(cur_eng[0], halo[:, ci, :], hbuf[:, n:n + 2], [hbuf.b], [halo_b[ci]])
                                    if not is_meta:
                                        dgs = dg[i % 2][which]
                                        for tap in range(3):
                                            P.ts("dve", dgs[tap][:], identb[:], cwb[:, ci, tap:tap + 1], None, ALU.mult, None,
                                                 [identb.b, cwb.b], [dgs[tap].b])
                                        pc = pring.next()
                                        for tap in range(3):
                                            P.mm(pc.h[:, 0:n], dgs[tap][:], hbuf[:, tap:tap + n], tap == 0, tap == 2,
                                                 [dgs[tap].b, hbuf.b], [pc.b])
                                        if which == 0:
                                            P.act(dst[:, 0:n], pc.h[:, 0:n], AF.Identity, [pc.b, cwb.b], [dst.b], bias=cwb[:, ci, 3:4])
                                        else:
                                            P.act(sgb[:, 0:n], pc.h[:, 0:n], AF.Silu, [pc.b, cwb.b], [sgb.b], bias=cwb[:, ci, 3:4])
                                if not is_meta:
                                    P.tt("pool", at[:, il, 0:n], sgb[:, 0:n], cv[i % 2][:, 0:n], ALU.mult, [sgb.b, cv[i % 2].b], [at.b])
                            if not is_meta and uu == 0 and g >= 1:
                                ffn_down(g - 1)
                        if not is_meta and g == 5:
                            ffn_down(5)
                    ckpt("s7_%d_%d" % (seq, pos0))
                    if is_meta:
                        P.cp("dve", snapF[:], Fcar[:], [Fcar.b], [snapF.b])
                        P.cp("dve", snapZ[:], Zs[zcur[0]][:], [Zs[zcur[0]].b], [snapZ.b])
                        P.cp("dve", snapH[:], halo[:], halo_b, [snapH.b])
                    if not is_meta:
                        P.memset("pool", ss[:], 0.0, [ss.b])
                        for t in range(nt):
                            P.act(junk[0:tp, :], xtok[0:tp, t, :], AF.Square, [xtok.b, ss.b], [junk.b, ss.b], accum_out=ss[0:tp, t:t + 1])
                        P.act(rstd[0:tp, 0:nt], ss[0:tp, 0:nt], AF.Sqrt, [ss.b], [rstd.b], scale=1.0 / D, bias=EPS)
                        P.op("dve", lambda e: e.reciprocal(out=rstd[0:128, 0:4], in_=rstd[0:128, 0:4]), [rstd.b], [rstd.b])
                        for t in range(nt):
                            P.stt("dve", xtok[0:tp, t, :], xtok[0:tp, t, :], rstd[0:tp, t:t + 1], gfin[0:tp, :], ALU.mult, ALU.mult,
                                  [xtok.b, rstd.b, gfin.b], [xtok.b])
                        P.dma("pool", y[seq, bi * 512:(bi + 1) * 512, :].rearrange("(t p) d -> p t d", p=128), xtok[:, :, :], reads=[xtok.b], writes=[Buf()])

        except _Stop:
            pass
        SBUF_LEFT.append(nc.sbuf_bytes_remaining)
        P.barrier()
        P.emit()
    return nc


_CACHE = {}


def _consts():
    ident = np.eye(128, dtype=np.float32)
    k = np.arange(128)[:, None]
    q = np.arange(128)[None, :]
    mask = np.where(k <= q, 0.0, MASKV).astype(np.float32)
    sel = np.zeros((8, 8, 128), dtype=np.float32)
    for h in range(8):
        sel[h, h, :] = 1.0
    bd = (k // 16 == q // 16).astype(np.float32)
    return {"c_ident": ident, "c_mask": mask, "c_sel": sel, "c_bd": bd}


def kernel(**inputs):
    if "nc" not in _CACHE:
        _CACHE["nc"] = build_program()
    nc = _CACHE["nc"]
    consts = _consts()
    ncores = 8
    in_maps = []
    for c in range(ncores):
        m = {}
        for k_, v in inputs.items():
            a = np.ascontiguousarray(np.asarray(v, dtype=np.float32))
            if k_ == "x":
                a = np.ascontiguousarray(a[NSEQ * c:NSEQ * (c + 1)])
            m[k_] = a
        m.update(consts)
        in_maps.append(m)
    res = run_bass_kernel_spmd(nc, in_maps, core_ids=list(range(ncores)))
    out = np.concatenate([np.asarray(r["y"], dtype=np.float32) for r in res.results], axis=0)
    return out
```

```python
import math
import os
import contextlib
DBG = os.environ.get('DBG', '')
import numpy as np
import concourse.bass as bass
import concourse.mybir as mybir
from concourse.bass_utils import run_bass_kernel_spmd

F32 = mybir.dt.float32
BF16 = mybir.dt.bfloat16
I32 = mybir.dt.int32
AF = mybir.ActivationFunctionType
ALU = mybir.AluOpType

D = 1024
KT = 8
H = 8
NMETA = 16
SEQ = 2048
L = NMETA + SEQ
DFF = 2816
NFT = 22
TCH = 8
NG = 32
NSTRIP = 16
EPS = 1e-6
NSEQ = 2
IN_W = 4104
MASKV = -30000.0


class Buf:
    __slots__ = ("writers", "readers", "excl")

    def __init__(self, excl=False):
        self.writers = {}
        self.readers = {}
        self.excl = excl


class Prog:
    ENG = ("pe", "act", "dve", "pool", "sp")

    def __init__(self, nc, n_streams=16):
        self.nc = nc
        self.ops = {e: [] for e in self.ENG}
        self.cnt = {e: 0 for e in self.ENG}
        self.n_streams = n_streams
        self.stream_cnt = [0] * n_streams
        self.waited = {e: {} for e in self.ENG}
        self.groups = {"sp": list(range(0, 8)), "pool": list(range(8, 12)), "cast": list(range(12, 16))}
        self.rr = {"sp": 0, "pool": 0, "cast": 0}

    def _deps(self, eng, reads, writes, skip_pe=False):
        deps = {}

        def add(d):
            if d is None:
                return
            k, v = d
            if deps.get(k, 0) < v:
                deps[k] = v
        for b in reads:
            for d in b.writers.items():
                add(d)
        for b in writes:
            for d in b.writers.items():
                add(d)
            for d in b.readers.items():
                add(d)
        waits = []
        w = self.waited[eng]
        for k, v in deps.items():
            if skip_pe and k == "pe":
                continue
            if w.get(k, 0) >= v:
                continue
            w[k] = v
            waits.append((k, v))
        return waits

    def _commit(self, me, reads, writes):
        k, v = me
        for b in reads:
            if b.readers.get(k, 0) < v:
                b.readers[k] = v
        for b in writes:
            if b.writers.get(k, 0) < v:
                b.writers[k] = v

    def op(self, eng, fn, reads=(), writes=(), pe_mm=False):
        ex = [b for b in reads if b.excl]
        if ex:
            writes = list(writes) + ex
        waits = self._deps(eng, reads, writes, skip_pe=pe_mm)
        self.cnt[eng] += 1
        me = (eng, self.cnt[eng])
        self.ops[eng].append((waits, fn, ("c", eng)))
        self._commit(me, reads, writes)
        return me

    def dma(self, eng, out, in_, reads=(), writes=(), cast=False, **kw):
        grp = "cast" if cast else eng
        lst = self.groups[grp]
        stream = lst[self.rr[grp]]
        self.rr[grp] = (self.rr[grp] + 1) % len(lst)
        key = ("d", stream)
        waits = self._deps(eng, reads, writes)
        prev = self.stream_cnt[stream] * 16
        if prev and self.waited[eng].get(key, 0) < prev:
            self.waited[eng][key] = prev
            waits.append((key, prev))
        self.stream_cnt[stream] += 1
        me = (key, self.stream_cnt[stream] * 16)
        self.ops[eng].append((waits, lambda e, o=out, i=in_, kw=kw: e.dma_start(out=o, in_=i, **kw), ("d", stream)))
        self._commit(me, reads, writes)
        return me

    def barrier(self, skip_cast=False):
        targets = [(e, self.cnt[e]) for e in self.ENG if self.cnt[e]]
        targets += [(("d", s), self.stream_cnt[s] * 16) for s in range(self.n_streams)
                    if self.stream_cnt[s] and not (skip_cast and s in self.groups["cast"])]
        for e in self.ENG:
            waits = []
            for k, v in targets:
                if self.waited[e].get(k, 0) < v:
                    self.waited[e][k] = v
                    waits.append((k, v))
            if waits:
                self.ops[e].append((waits, None, None))

    def act(self, out, in_, func, reads, writes, **kw):
        return self.op("act", lambda e: e.activation(out=out, in_=in_, func=func, **kw), reads, writes)

    def tt(self, eng, out, in0, in1, op, reads, writes):
        return self.op(eng, lambda e: e.tensor_tensor(out=out, in0=in0, in1=in1, op=op), reads, writes)

    def ts(self, eng, out, in0, s1, s2, op0, op1, reads, writes):
        if s2 is None:
            return self.op(eng, lambda e: e.tensor_scalar(out=out, in0=in0, scalar1=s1, scalar2=None, op0=op0), reads, writes)
        return self.op(eng, lambda e: e.tensor_scalar(out=out, in0=in0, scalar1=s1, scalar2=s2, op0=op0, op1=op1), reads, writes)

    def stt(self, eng, out, in0, scalar, in1, op0, op1, reads, writes):
        return self.op(eng, lambda e: e.scalar_tensor_tensor(out=out, in0=in0, scalar=scalar, in1=in1, op0=op0, op1=op1), reads, writes)

    def cp(self, eng, out, in_, reads, writes):
        if eng == "act":
            return self.op("act", lambda e: e.activation(out=out, in_=in_, func=AF.Copy), reads, writes)
        return self.op(eng, lambda e: e.tensor_copy(out=out, in_=in_), reads, writes)

    def memset(self, eng, ap, val, writes):
        return self.op(eng, lambda e: e.memset(ap, val), (), writes)

    def mm(self, out, lhsT, rhs, start, stop, reads, writes, tile_position=None):
        if tile_position is None:
            fn = lambda e: e.matmul(out, lhsT=lhsT, rhs=rhs, start=start, stop=stop)
        else:
            fn = lambda e: e.matmul(out, lhsT=lhsT, rhs=rhs, start=start, stop=stop, tile_position=tile_position)
        return self.op("pe", fn, reads, writes, pe_mm=True)

    def tr(self, out, in_, ident, reads, writes):
        return self.op("pe", lambda e: e.transpose(out=out, in_=in_, identity=ident), reads, writes, pe_mm=True)

    def emit(self):
        nc = self.nc
        with contextlib.ExitStack() as st:
            sems = {}
            for e in self.ENG:
                sems[e] = st.enter_context(nc.semaphore("s_" + e))
            for i in range(self.n_streams):
                sems[("d", i)] = st.enter_context(nc.semaphore("d%d" % i))
            block = st.enter_context(nc.Block())
            engmap = {"pe": "tensor", "act": "scalar", "dve": "vector", "pool": "gpsimd", "sp": "sync"}

            def make(ename):
                oplist = self.ops[ename]

                def body(eng):
                    for waits, fn, inc in oplist:
                        for k, v in waits:
                            eng.wait_ge(sems[k], v)
                        if fn is None:
                            continue
                        ins = fn(eng)
                        if inc[0] == "c":
                            ins.then_inc(sems[inc[1]], 1)
                        else:
                            ins.then_inc(sems[inc], 16)
                return body

            for e in self.ENG:
                if self.ops[e]:
                    getattr(block, engmap[e])(make(e))


class TB:
    def __init__(self, h):
        self.h = h
        self.b = Buf()

    def __getitem__(self, k):
        return self.h[k]


class Ring:
    def __init__(self, items):
        self.items = items
        self.i = 0

    def next(self):
        it = self.items[self.i]
        self.i = (self.i + 1) % len(self.items)
        return it


class _Stop(Exception):
    pass


CKPTS = []
SBUF_LEFT = []


def build_program(debug=False, limit=None):
    nc = bass.Bass("TRN2", target_bir_lowering=False)
    dr = lambda name, shape, dt=F32, kind="ExternalInput": nc.dram_tensor(name, list(shape), dt, kind=kind).ap()
    x = dr("x", [NSEQ, SEQ, D])
    meta_tokens = dr("meta_tokens", [NMETA, D])
    norm_mix_g = dr("norm_mix_g", [1, D])
    w_in = dr("w_in", [1, D, IN_W])
    b_forget = dr("b_forget", [1, H])
    w_attn_out = dr("w_attn_out", [1, 512, D])
    lam_re = dr("ssm_lambda_re", [1, NG, 64])
    lam_im = dr("ssm_lambda_im", [1, NG, 64])
    b_re = dr("ssm_b_re", [1, NG, 64, 16])
    b_im = dr("ssm_b_im", [1, NG, 64, 16])
    c_re = dr("ssm_c_re", [1, NG, 16, 64])
    c_im = dr("ssm_c_im", [1, NG, 16, 64])
    ssm_d = dr("ssm_d", [1, 512])
    log_dt = dr("ssm_log_dt", [1, NG])
    w_glu = dr("w_glu", [1, 512, 2 * D])
    w_o = dr("w_o", [1, D, D])
    norm_ffn_g = dr("norm_ffn_g", [1, D])
    w_up = dr("w_ffn_up", [1, D, 2 * DFF])
    conv_w = dr("ffn_conv_w", [1, 3, 2 * DFF])
    conv_b = dr("ffn_conv_b", [1, 2 * DFF])
    w_down = dr("w_ffn_down", [1, DFF, D])
    norm_final_g = dr("norm_final_g", [D])
    c_ident = dr("c_ident", [128, 128])
    c_mask = dr("c_mask", [128, 128])
    c_sel = dr("c_sel", [8, 8, 128])
    c_bd = dr("c_bd", [128, 128])
    y = dr("y", [NSEQ, SEQ, D], kind="ExternalOutput")
    ws_in = dr("ws_in", [4, 128, 8, 512], BF16, "Internal")
    ws_f = dr("ws_f", [128, 8, 8], BF16, "Internal")
    ws_gm = dr("ws_gm", [4, 128, 8, 512], BF16, "Internal")
    ws_zm = dr("ws_zm", [4, 128, 4, 512], BF16, "Internal")
    ws_am = dr("ws_am", [4, 128, 4, 256], BF16, "Internal")
    ws_o = dr("ws_o", [2, 128, 8, 512], BF16, "Internal")
    ws_up = dr("ws_up", [11, 128, 8, 512], BF16, "Internal")
    ws_dn = dr("ws_dn", [6, 128, 4, 1024], BF16, "Internal")

    st = contextlib.ExitStack()
    with st:
        def sb(name, shape, dt):
            return TB(st.enter_context(nc.sbuf_tensor(name, list(shape), dt)))

        def psb(name, shape, dt):
            t = TB(st.enter_context(nc.psum_tensor(name, list(shape), dt)))
            t.b.excl = True
            return t

        P = Prog(nc)

        def ckpt(name):
            CKPTS.append((name, dict(P.cnt)))
            if limit is not None and name == limit:
                if DBG.startswith('pad'):
                    eng_, n_ = DBG[3:].split(':')
                    for _ in range(int(n_)):
                        if eng_ == 'pe':
                            P.mm(pbanks[0].h[:, 0:128], identb[:, :], identb[:, :], True, True, [identb.b], [pbanks[0].b])
                        else:
                            P.memset(eng_, zt1[:], 0.0, [zt1.b])
                raise _Stop()

        identf = sb("identf", [128, 128], F32)
        identb = sb("identb", [128, 128], BF16)
        maskb = sb("maskb", [128, 128], BF16)
        selb = sb("selb", [128, 8, 128], BF16)
        bdm = sb("bdm", [128, 128], F32)
        ones = sb("ones", [128, 512], F32)
        gmix = sb("gmix", [128, 8], F32)
        gffn = sb("gffn", [128, 8], F32)
        gfin = sb("gfin", [128, D], F32)
        dcol = sb("dcol", [128, 4], F32)
        negb = sb("negb", [8, 1], F32)
        cwb = sb("cwb", [128, 44, 4], F32)
        wf = sb("wf", [128, 8, 8], BF16)
        KA = sb("KA", [128, 4, L], BF16)
        Vc = sb("Vc", [128, 17, 4, 192], BF16)
        FT = sb("FT", [128, 17, 8], F32)
        LagK = sb("LagK", [128, 4, TCH, 128], BF16)
        Gt = sb("Gt", [128, 4, TCH, 2, 128], BF16)
        Hst = sb("Hst", [128, TCH, 2, 512], BF16)
        CA = sb("CA", [128, 2, NSTRIP], F32)
        CB = sb("CB", [128, 2, NSTRIP], F32)
        xtok = sb("xtok", [128, 4, D], F32)
        xnb = [sb("xnb%d" % i, [128, D], BF16) for i in range(2)]
        ss = sb("ss", [128, 4], F32)
        ss2 = sb("ss2", [128, 4], F32)
        rstd2 = sb("rstd2", [128, 4], F32)
        rstd = sb("rstd", [128, 4], F32)
        xnT = sb("xnT", [128, 8, 512], BF16)
        xn2T = xnT
        QA = sb("QA", [128, 8, 512], BF16)
        QF = sb("QF", [128, 512], BF16)
        Fcar = sb("Fcar", [8, 1], F32)
        UT = sb("UT", [128, 4, 512], BF16)
        attnT = sb("attnT", [128, 4, 512], BF16)
        zT = sb("zT", [128, 4, 512], BF16)
        R8 = sb("R8", [128, 4096], BF16)
        R8lo = Buf()
        R8hi = Buf()
        R8b = [R8lo, R8hi]

        class View:
            def __init__(self, ap, b):
                self.ap = ap
                self.b = b

            def __getitem__(self, k):
                return self.ap[k]

        junk = View(R8.h[:, 0:1024], R8lo)
        mergedT = View(R8.h[:, :].rearrange("p (m n) -> p m n", n=512), None)
        Vsb = View(R8.h[:, :].bitcast(F32).rearrange("p (r s j) -> p r s j", r=2, s=NSTRIP), None)
        actT = [View(R8.h[:, 2048 * i:2048 * i + 2048].rearrange("p (m n) -> p m n", n=512), R8b[i]) for i in range(2)]
        PT = [sb("PT%d" % i, [128, 512], BF16) for i in range(3)]
        Sprev = sb("Sprev", [128, 2, NSTRIP, 64], BF16)
        Zs = [sb("Zs%d" % i, [128, 2, NSTRIP], F32) for i in range(2)]
        zt1 = sb("zt1", [128, 2, NSTRIP], F32)
        zt2 = sb("zt2", [128, 2, NSTRIP], F32)
        sig = [[sb("sig%d_%d" % (i, j), [128, 512], BF16) for j in range(3)] for i in range(2)]
        ft = [sb("ft%d" % i, [128, 512], F32) for i in range(5)]
        ysb, t1b, t2b = ft[0], ft[1], ft[2]
        cv = [ft[0], ft[1]]
        cg = [ft[2], ft[3]]
        sgb = ft[4]
        rec = [ft[3], ft[4]]
        fe = View(ft[0].h[0:8, :], ft[0].b)
        Fp = View(ft[1].h[0:8, :], ft[1].b)
        halo = sb("halo", [128, 44, 2], F32)
        halo_b = [Buf() for _ in range(44)]
        snapF = sb("snapF", [8, 1], F32)
        snapZ = sb("snapZ", [128, 2, NSTRIP], F32)
        snapH = sb("snapH", [128, 44, 2], F32)
        hb = [[sb("hb%d_%d" % (w_, i_), [128, 516], BF16) for i_ in range(2)] for w_ in range(2)]
        dg = [[[sb("dg%d_%d_%d" % (p_, w_, t_), [128, 128], BF16) for t_ in range(3)] for w_ in range(2)] for p_ in range(2)]
        NSLOT = 3
        slots = [sb("slot%d" % i, [128, 4096], BF16) for i in range(NSLOT)]
        slot_ring = Ring(slots)
        pbanks = [psb("pb%d" % i, [128, 512], F32) for i in range(6)]
        pring = Ring(pbanks)
        obanks = [psb("ob%d" % i, [128, 512], F32) for i in range(2)]
        oring = Ring(obanks)

        def bank_bf16(pb):
            return pb.h[:].bitcast(BF16)

        kp = lambda ap: ap.rearrange("(k p) j -> p k j", p=128)
        GB = 2056
        cast_bufs = {}

        def ensure_cast(name, c):
            key = (name, c)
            if key in cast_bufs:
                return cast_bufs[key]
            b = Buf()
            cast_bufs[key] = [b]
            cd = lambda o, i: P.dma("pool", o, i, writes=[b], cast=True)
            if name == "in":
                lo = [0, 512, 1024, 1544][c]
                cd(ws_in[c], kp(w_in[0][:, lo:lo + 512]))
            elif name == "f":
                cd(ws_f, kp(w_in[0][:, 1536:1544]))
            elif name == "gm":
                cd(ws_gm[c][:, :, 0:256], kp(w_in[0][:, GB + 256 * c: GB + 256 * c + 256]))
                cd(ws_gm[c][:, :, 256:512], kp(w_in[0][:, GB + 1024 + 256 * c: GB + 1024 + 256 * c + 256]))
            elif name == "zm":
                cd(ws_zm[c][:, :, 0:256], kp(w_glu[0][:, 256 * c: 256 * c + 256]))
                cd(ws_zm[c][:, :, 256:512], kp(w_glu[0][:, 1024 + 256 * c: 1024 + 256 * c + 256]))
            elif name == "am":
                cd(ws_am[c], kp(w_attn_out[0][:, 256 * c: 256 * c + 256]))
            elif name == "o":
                cd(ws_o[c], kp(w_o[0][:, 512 * c: 512 * c + 512]))
            elif name == "up":
                cd(ws_up[c][:, :, 0:256], kp(w_up[0][:, 256 * c: 256 * c + 256]))
                cd(ws_up[c][:, :, 256:512], kp(w_up[0][:, DFF + 256 * c: DFF + 256 * c + 256]))
            elif name == "dn":
                nt_ = 4 if c < 5 else 2
                cd(ws_dn[c][:, 0:nt_, :], w_down[0][512 * c: 512 * c + 128 * nt_, :].rearrange("(t p) j -> p t j", p=128))
            return cast_bufs[key]

        tmpb = Buf()

        def xt_tmp(i):
            return xtok.h[:, i // 2, (i % 2) * 512:(i % 2) * 512 + 512]

        def slot_tmp(i):
            return slots[i // 4].h[:, (i % 4) * 1024:(i % 4) * 1024 + 1024].bitcast(F32)

        class Tmp:
            def __init__(self, ap):
                self.ap = ap
                self.b = Buf()

            def g(self):
                return self.ap.rearrange("p (g c) -> p g c", c=16)

            def s(self):
                return self.ap[:, 0:32]

        def xn_tmp(i):
            return xnT.h[:, 2 * i:2 * i + 2, :].rearrange("p a b -> p (a b)").bitcast(F32)

        tmps = [Tmp(xt_tmp(i)) for i in range(8)] + [Tmp(slot_tmp(i)) for i in range(4 * NSLOT)] + [Tmp(xn_tmp(i)) for i in range(4)]
        free_tmps = list(tmps)

        def newtmp():
            return free_tmps.pop(0)

        def rel(*ts_):
            for t in ts_:
                free_tmps.append(t)

        try:
            P.dma("sp", identf[:], c_ident, writes=[identf.b])
            P.cp("dve", identb[:], identf[:], [identf.b], [identb.b])
            t_m = newtmp()
            P.dma("sp", t_m.ap[:, 0:128], c_mask, writes=[t_m.b])
            P.cp("dve", maskb[:], t_m.ap[:, 0:128], [t_m.b], [maskb.b])
            csel2 = c_sel.rearrange("r h k -> r (h k)")
            P.memset("dve", selb[:], 0.0, [selb.b])
            P.memset("dve", QA[:], 0.0, [QA.b])
            P.memset("dve", QF[:], 0.0, [QF.b])
            for hh in range(2):
                P.dma("sp", t_m.ap[0:8, :], csel2[:, 512 * hh:512 * hh + 512], reads=[], writes=[t_m.b])
                P.cp("dve", selb[0:8, 4 * hh:4 * hh + 4, :], t_m.ap[0:8, :].rearrange("p (h k) -> p h k", k=128), [t_m.b], [selb.b])
            P.dma("sp", bdm[:], c_bd, writes=[bdm.b])
            P.memset("pool", ones[:], 1.0, [ones.b])
            P.dma("sp", gfin[:], norm_final_g.partition_broadcast(128), writes=[gfin.b])
            P.dma("sp", wf[:], ws_f, reads=ensure_cast("f", 0), writes=[wf.b])
            P.memset("pool", Vc[:], 1.0, [Vc.b])
            P.memset("pool", FT[:], 0.0, [FT.b])

            for dst, src, r in ((gmix, norm_mix_g, 8), (gffn, norm_ffn_g, 8), (dcol, ssm_d, 4)):
                P.dma("sp", t_m.ap[0:r, 0:128], src[0].rearrange("(k p) -> k p", p=128), writes=[t_m.b])
                pb = pring.next()
                P.tr(pb.h[:, 0:r], t_m.ap[0:r, 0:128], identf[0:r, 0:r], [t_m.b, identf.b], [pb.b])
                P.cp("dve", dst[:], pb.h[:, 0:r], [pb.b], [dst.b])
            for part in range(11):
                P.dma("sp", t_m.ap[0:3, :], conv_w[0][:, part * 512:(part + 1) * 512], writes=[t_m.b])
                P.dma("sp", t_m.ap[3:4, :], conv_b[0][part * 512:(part + 1) * 512].rearrange("(o n) -> o n", o=1), writes=[t_m.b])
                pb = pring.next()
                for j in range(4):
                    P.tr(pb.h[:, 4 * j:4 * j + 4], t_m.ap[0:4, j * 128:(j + 1) * 128], identf[0:4, 0:4], [t_m.b, identf.b], [pb.b])
                P.cp("dve", cwb[:, 4 * part:4 * part + 4, :], pb.h[:, 0:16].rearrange("p (i j) -> p i j", j=4), [pb.b], [cwb.b])
            P.dma("sp", negb[:], b_forget[0].rearrange("(h o) -> h o", o=1), writes=[negb.b])
            P.ts("dve", negb[:], negb[:], -1.0, None, ALU.mult, None, [negb.b], [negb.b])

            def dup_T(src):
                P.dma("sp", t_m.ap[0:32, 0:64], src, writes=[t_m.b])
                P.dma("sp", t_m.ap[0:32, 64:128], src, writes=[t_m.b])
                pb = pring.next()
                P.tr(pb.h[:, 0:32], t_m.ap[0:32, 0:128], identf[0:32, 0:32], [t_m.b, identf.b], [pb.b])
                o = newtmp()
                P.cp("dve", o.s(), pb.h[:, 0:32], [pb.b], [o.b])
                return o

            LR = dup_T(lam_re[0])
            LI = dup_T(lam_im[0])
            DT = newtmp()
            P.dma("sp", DT.s(), log_dt[0].partition_broadcast(128), writes=[DT.b])
            P.act(DT.s(), DT.s(), AF.Exp, [DT.b], [DT.b])
            PM = newtmp()
            P.memset("pool", PM.s(), 0.0, [PM.b])
            P.memset("pool", PM.ap[0:64, 0:32].rearrange("p (s two) -> p s two", two=2)[:, :, 0], 1.0, [PM.b])
            P.memset("pool", PM.ap[64:128, 0:32].rearrange("p (s two) -> p s two", two=2)[:, :, 1], 1.0, [PM.b])
            for _c in range(4):
                ensure_cast("in", _c)
            for _c in range(4):
                ensure_cast("gm", _c)
                ensure_cast("zm", _c)
                ensure_cast("am", _c)
            for _c in range(2):
                ensure_cast("o", _c)
            for _g in range(6):
                ensure_cast("up", 2 * _g)
                if _g < 5:
                    ensure_cast("up", 2 * _g + 1)
                ensure_cast("dn", _g)

            TWO_PI = 2.0 * math.pi

            def sin_of(dst, src, shift, scratch_f, scratch_i):
                a = dst.s()
                kf = scratch_f.s()
                ki = scratch_i.ap[:, 0:32].bitcast(I32)
                P.ts("dve", a, src.s(), 1.0, shift, ALU.mult, ALU.add, [src.b], [dst.b])
                P.ts("dve", kf, a, 1.0 / TWO_PI, None, ALU.mult, None, [dst.b], [scratch_f.b])
                P.cp("dve", ki, kf, [scratch_f.b], [scratch_i.b])
                P.cp("dve", kf, ki, [scratch_i.b], [scratch_f.b])
                P.stt("dve", a, kf, -TWO_PI, a, ALU.mult, ALU.add, [scratch_f.b, dst.b], [dst.b])
                P.ts("dve", kf, a, -math.pi, None, ALU.is_lt, None, [dst.b], [scratch_f.b])
                P.stt("dve", a, kf, TWO_PI, a, ALU.mult, ALU.add, [scratch_f.b, dst.b], [dst.b])
                P.ts("dve", kf, a, math.pi, None, ALU.is_gt, None, [dst.b], [scratch_f.b])
                P.stt("dve", a, kf, -TWO_PI, a, ALU.mult, ALU.add, [scratch_f.b, dst.b], [dst.b])
                P.ts("dve", a, a, -3.141592, 3.141592, ALU.max, ALU.min, [dst.b], [dst.b])
                P.act(a, a, AF.Sin, [dst.b], [dst.b])

            AR = newtmp()
            AI = newtmp()
            P.tt("dve", AR.s(), LR.s(), DT.s(), ALU.mult, [LR.b, DT.b], [AR.b])
            P.tt("dve", AI.s(), LI.s(), DT.s(), ALU.mult, [LI.b, DT.b], [AI.b])
            MAG = newtmp()
            P.act(MAG.s(), AR.s(), AF.Exp, [AR.b], [MAG.b])
            SN = newtmp()
            CS = newtmp()
            SC1 = newtmp()
            SC2 = newtmp()
            sin_of(SN, AI, 0.0, SC1, SC2)
            sin_of(CS, AI, math.pi / 2, SC1, SC2)
            LBr = AR
            LBi = AI
            P.tt("dve", LBr.s(), MAG.s(), CS.s(), ALU.mult, [MAG.b, CS.b], [LBr.b])
            P.tt("dve", LBi.s(), MAG.s(), SN.s(), ALU.mult, [MAG.b, SN.b], [LBi.b])
            NR = MAG
            P.ts("dve", NR.s(), LBr.s(), -1.0, None, ALU.add, None, [LBr.b], [NR.b])
            DEN = SN
            P.tt("dve", DEN.s(), LR.s(), LR.s(), ALU.mult, [LR.b], [DEN.b])
            P.tt("dve", SC1.s(), LI.s(), LI.s(), ALU.mult, [LI.b], [SC1.b])
            P.tt("dve", DEN.s(), DEN.s(), SC1.s(), ALU.add, [DEN.b, SC1.b], [DEN.b])
            P.op("dve", lambda e: e.reciprocal(out=DEN.s(), in_=DEN.s()), [DEN.b], [DEN.b])
            KR = CS
            KI = SC2
            P.tt("dve", KR.s(), NR.s(), LR.s(), ALU.mult, [NR.b, LR.b], [KR.b])
            P.tt("dve", SC1.s(), LBi.s(), LI.s(), ALU.mult, [LBi.b, LI.b], [SC1.b])
            P.tt("dve", KR.s(), KR.s(), SC1.s(), ALU.add, [KR.b, SC1.b], [KR.b])
            P.tt("dve", KR.s(), KR.s(), DEN.s(), ALU.mult, [KR.b, DEN.b], [KR.b])
            P.tt("dve", KI.s(), LBi.s(), LR.s(), ALU.mult, [LBi.b, LR.b], [KI.b])
            P.tt("dve", SC1.s(), NR.s(), LI.s(), ALU.mult, [NR.b, LI.b], [SC1.b])
            P.tt("dve", KI.s(), KI.s(), SC1.s(), ALU.subtract, [KI.b, SC1.b], [KI.b])
            P.tt("dve", KI.s(), KI.s(), DEN.s(), ALU.mult, [KI.b, DEN.b], [KI.b])
            P.tt("dve", KR.s(), KR.s(), PM.s(), ALU.mult, [KR.b, PM.b], [KR.b])
            P.tt("dve", KI.s(), KI.s(), PM.s(), ALU.mult, [KI.b, PM.b], [KI.b])
            rel(LR, LI, DT, MAG, SN, SC1)

            def bc(t):
                return t.s().unsqueeze(2).to_broadcast([128, 32, 16])

            Br = newtmp()
            Bi = newtmp()
            for dst, src in ((Br, b_re), (Bi, b_im)):
                for half in range(2):
                    for q4 in range(4):
                        P.dma("sp", dst.g()[64 * half:64 * half + 64, 8 * q4:8 * q4 + 8, :],
                              src[0][8 * q4:8 * q4 + 8].rearrange("g p c -> p g c"), writes=[dst.b])
            Er = newtmp()
            Ei = newtmp()
            W1 = newtmp()
            W2 = newtmp()
            P.tt("dve", Er.g(), Br.g(), bc(KR), ALU.mult, [Br.b, KR.b], [Er.b])
            P.tt("dve", W1.g(), Bi.g(), bc(KI), ALU.mult, [Bi.b, KI.b], [W1.b])
            P.tt("dve", Er.g(), Er.g(), W1.g(), ALU.subtract, [Er.b, W1.b], [Er.b])
            P.tt("dve", Ei.g(), Bi.g(), bc(KR), ALU.mult, [Bi.b, KR.b], [Ei.b])
            P.tt("dve", W1.g(), Br.g(), bc(KI), ALU.mult, [Br.b, KI.b], [W1.b])
            P.tt("dve", Ei.g(), Ei.g(), W1.g(), ALU.add, [Ei.b, W1.b], [Ei.b])
            rel(KR, KI)
            Fr = Br
            Fi = Bi
            tn = newtmp()
            for dst, src in ((Fr, c_re), (Fi, c_im)):
                for ct in range(4):
                    srcv = src[0][8 * ct:8 * ct + 8].rearrange("g c p -> (g c) p")
                    P.dma("sp", tn.ap[:, ct * 128:ct * 128 + 64], srcv, writes=[tn.b])
                    P.dma("sp", tn.ap[:, ct * 128 + 64:ct * 128 + 128], srcv, writes=[tn.b])
                pb = pring.next()
                for ct in range(4):
                    P.tr(pb.h[:, ct * 128:(ct + 1) * 128], tn.ap[:, ct * 128:(ct + 1) * 128], identf[:], [tn.b, identf.b], [pb.b])
                P.tt("dve", dst.g(), pb.h[:, :].rearrange("p (g c) -> p g c", c=16), bc(PM), ALU.mult, [pb.b, PM.b], [dst.b])
            rel(tn, PM)
            nFi0 = newtmp()
            P.ts("dve", nFi0.ap, Fi.ap, -1.0, None, ALU.mult, None, [Fi.b], [nFi0.b])
            Fr0 = newtmp()
            P.cp("dve", Fr0.ap, Fr.ap, [Fr.b], [Fr0.b])

            def cmul_step(Xr, Xi, Wa, Wb):
                P.tt("dve", Wa.g(), Xr.g(), bc(LBr), ALU.mult, [Xr.b, LBr.b], [Wa.b])
                P.tt("dve", Wb.g(), Xi.g(), bc(LBi), ALU.mult, [Xi.b, LBi.b], [Wb.b])
                P.tt("dve", Wa.g(), Wa.g(), Wb.g(), ALU.subtract, [Wa.b, Wb.b], [Wa.b])
                P.tt("dve", Wb.g(), Xr.g(), bc(LBi), ALU.mult, [Xr.b, LBi.b], [Wb.b])
                P.tt("dve", Xi.g(), Xi.g(), bc(LBr), ALU.mult, [Xi.b, LBr.b], [Xi.b])
                P.tt("dve", Xi.g(), Xi.g(), Wb.g(), ALU.add, [Xi.b, Wb.b], [Xi.b])
                P.cp("dve", Xr.ap, Wa.ap, [Wa.b], [Xr.b])

            for n in range(TCH):
                for ri, E in ((0, Er), (1, Ei)):
                    pb = pring.next()
                    for ct in range(4):
                        P.tr(pb.h[:, ct * 128:(ct + 1) * 128], E.ap[:, ct * 128:(ct + 1) * 128], identf[:], [E.b, identf.b], [pb.b])
                    P.cp("act", Gt[:, :, n, ri, :], pb.h[:, :].rearrange("p (t c) -> p t c", c=128), [pb.b], [Gt.b])
                pb = pring.next()
                for ct in range(4):
                    cs_ = slice(ct * 128, (ct + 1) * 128)
                    P.mm(pb.h[:, cs_], Er.ap[:, cs_], Fr0.ap[:, cs_], True, False, [Er.b, Fr0.b], [pb.b])
                    P.mm(pb.h[:, cs_], Ei.ap[:, cs_], nFi0.ap[:, cs_], False, True, [Ei.b, nFi0.b], [pb.b])
                for ct in range(4):
                    cs_ = slice(ct * 128, (ct + 1) * 128)
                    if n == 0:
                        P.tt("dve", W1.ap[:, 0:128], pb.h[:, cs_], bdm[:], ALU.mult, [pb.b, bdm.b], [W1.b])
                        P.stt("dve", LagK[:, ct, 0, :], identf[:], dcol[:, ct:ct + 1], W1.ap[:, 0:128], ALU.mult, ALU.add,
                              [identf.b, dcol.b, W1.b], [LagK.b])
                    else:
                        P.tt("dve", LagK[:, ct, n, :], pb.h[:, cs_], bdm[:], ALU.mult, [pb.b, bdm.b], [LagK.b])
                if n < TCH - 1:
                    cmul_step(Er, Ei, W1, W2)
            for n in range(1, TCH + 1):
                cmul_step(Fr, Fi, W1, W2)
                P.cp("act", Hst[:, n - 1, 0, :], Fr.ap, [Fr.b], [Hst.b])
                P.ts("dve", Hst[:, n - 1, 1, :], Fi.ap, -1.0, None, ALU.mult, None, [Fi.b], [Hst.b])
            Pr = newtmp()
            Pi = newtmp()
            P.cp("dve", Pr.s(), LBr.s(), [LBr.b], [Pr.b])
            P.cp("dve", Pi.s(), LBi.s(), [LBi.b], [Pi.b])
            for _ in range(3):
                P.tt("dve", W1.s(), Pr.s(), Pr.s(), ALU.mult, [Pr.b], [W1.b])
                P.tt("dve", W2.s(), Pi.s(), Pi.s(), ALU.mult, [Pi.b], [W2.b])
                P.tt("dve", W1.s(), W1.s(), W2.s(), ALU.subtract, [W1.b, W2.b], [W1.b])
                P.tt("dve", W2.s(), Pr.s(), Pi.s(), ALU.mult, [Pr.b, Pi.b], [W2.b])
                P.ts("dve", Pi.s(), W2.s(), 2.0, None, ALU.mult, None, [W2.b], [Pi.b])
                P.cp("dve", Pr.s(), W1.s(), [W1.b], [Pr.b])
            for half in range(2):
                hs = slice(64 * half, 64 * half + 64)
                srcr = Pr.ap[hs, 0:32].rearrange("p (s two) -> p s two", two=2)[:, :, half]
                srci = Pi.ap[hs, 0:32].rearrange("p (s two) -> p s two", two=2)[:, :, half]
                P.cp("dve", CA[hs, 0, :], srcr, [Pr.b], [CA.b])
                P.cp("dve", CA[hs, 1, :], srcr, [Pr.b], [CA.b])
                P.ts("dve", CB[hs, 0, :], srci, -1.0, None, ALU.mult, None, [Pi.b], [CB.b])
                P.cp("dve", CB[hs, 1, :], srci, [Pi.b], [CB.b])

            ckpt("phase1")
            P.barrier()

            v8 = lambda s: s.h[:, :].rearrange("p (k j) -> p k j", j=512)
            v4 = lambda s: s.h[:, 0:2048].rearrange("p (k j) -> p k j", j=512)
            vam = lambda s: s.h[:, 2048:3072].rearrange("p (k j) -> p k j", j=256)
            vdn = lambda s: s.h[:, :].rearrange("p (t j) -> p t j", j=1024)

            WDEPTH = NSLOT - 2

            def block_reqs(is_meta):
                r = [[("in", c, ws_in[c], v8)] for c in range(4)]
                for m2 in range(4):
                    r.append([("gm", m2, ws_gm[m2], v8)])
                    r.append([("zm", m2, ws_zm[m2], v4), ("am", m2, ws_am[m2], vam)])
                r += [[("o", hh, ws_o[hh], v8)] for hh in range(2)]
                for g in range(6):
                    nunits = 2 if g < 5 else 1
                    def dnreq(gd):
                        nud = 2 if gd < 5 else 1
                        return [("dn", gd, ws_dn[gd][:, 0:2 * nud, :], (lambda s, nu=nud: vdn(s)[:, 0:2 * nu, :]))]
                    for uu in range(nunits):
                        r.append([("up", 2 * g + uu, ws_up[2 * g + uu], v8)])
                        if not is_meta and uu == 0 and g >= 1:
                            r.append(dnreq(g - 1))
                    if not is_meta and g == 5:
                        r.append(dnreq(5))
                return r

            all_reqs = []
            for _seq in range(NSEQ):
                for _m in ((True, False, False, False, False) if _seq == 0 else (False, False, False, False)):
                    all_reqs += block_reqs(_m)
            wq_state = [0, 0]

            def wnext(expect):
                i = wq_state[0]
                wq_state[0] += 1
                assert all_reqs[i][0][0] == expect[0] and all_reqs[i][0][1] == expect[1], (all_reqs[i][0][:2], expect)
                while wq_state[1] < len(all_reqs) and wq_state[1] <= i + WDEPTH:
                    j = wq_state[1]
                    slot = slots[j % NSLOT]
                    for (name, c, src, vf) in all_reqs[j]:
                        P.dma("sp", vf(slot), src, reads=ensure_cast(name, c), writes=[slot.b])
                    wq_state[1] += 1
                slot = slots[i % NSLOT]
                views = [vf(slot) for (_n, _c, _s, vf) in all_reqs[i]]
                return (slot, views[0]) if len(views) == 1 else (slot, views)

            def rms_stats(srcs, sbufs, ss_t, rstd_t, junk_ap, junk_bufs, tp, nt):
                P.memset(cur_eng[0], ss_t[:], 0.0, [ss_t.b])
                for t in range(nt):
                    P.act(junk_ap[0:tp, :], srcs[t][0:tp, :], AF.Square, [sbufs[t], ss_t.b], junk_bufs + [ss_t.b], accum_out=ss_t[0:tp, t:t + 1])
                P.act(rstd_t[0:tp, 0:nt], ss_t[0:tp, 0:nt], AF.Sqrt, [ss_t.b], [rstd_t.b], scale=1.0 / D, bias=EPS)
                P.op("dve", lambda e: e.reciprocal(out=rstd_t[0:tp, 0:nt], in_=rstd_t[0:tp, 0:nt]), [rstd_t.b], [rstd_t.b])

            def rms_T(srcs, sbufs, rstd_t, tp, nt, gcol, dstT):
                for t in range(nt):
                    xb_ = xnb[t % 2]
                    P.act(xb_[0:tp, :], srcs[t][0:tp, :], AF.Identity, [sbufs[t], rstd_t.b], [xb_.b], scale=rstd_t[0:tp, t:t + 1])
                    pb = pring.next()
                    pv_ = bank_bf16(pb).rearrange("p (k c) -> p k c", c=128)
                    for k in range(KT):
                        P.tr(pv_[:, k, 0:tp], xb_[0:tp, k * 128:(k + 1) * 128], identb[0:tp, 0:tp], [xb_.b, identb.b], [pb.b])
                    P.tt("dve", dstT[:, :, t * 128:t * 128 + tp], pv_[:, :, 0:tp],
                         gcol[:, :].unsqueeze(2).to_broadcast([128, KT, tp]), ALU.mult, [pb.b, gcol.b], [dstT.b])

            def rmsnorm_T(src_tb, n, tp, nt, gcol, dstT):
                srcs = [src_tb[:, t, :] for t in range(nt)]
                sbufs = [src_tb.b] * nt
                rms_stats(srcs, sbufs, ss, rstd, junk, [junk.b], tp, nt)
                rms_T(srcs, sbufs, rstd, tp, nt, gcol, dstT)

            f32v = lambda h_: h_.rearrange("p a b -> p (a b)").bitcast(F32)
            xp = [f32v(QA.h[:, 0:4, :]), f32v(QA.h[:, 4:8, :]), f32v(UT.h[:, :, :]), f32v(attnT.h[:, :, :])]
            xpb = [QA.b, QA.b, UT.b, attnT.b]
            xorder = [(sq_, bi_) for sq_ in range(NSEQ) for bi_ in range(4)]
            prefetched = set()

            zcur = [0]
            cur_eng = ["pool"]

            def scan_steps(j0, j1):
                for j in range(j0, j1):
                    Zc = Zs[zcur[0]]
                    Zn = Zs[1 - zcur[0]]
                    P.cp(cur_eng[0], Sprev[:, :, :, j], Zc[:], [Zc.b], [Sprev.b])
                    P.tt("dve", zt1[:], CA[:], Zc[:], ALU.mult, [CA.b, Zc.b], [zt1.b])
                    P.tt("dve", zt2[:, 0, :], CB[:, 0, :], Zc[:, 1, :], ALU.mult, [CB.b, Zc.b], [zt2.b])
                    P.tt("dve", zt2[:, 1, :], CB[:, 1, :], Zc[:, 0, :], ALU.mult, [CB.b, Zc.b], [zt2.b])
                    P.tt("dve", zt1[:], zt1[:], zt2[:], ALU.add, [zt1.b, zt2.b], [zt1.b])
                    P.tt("dve", Zn[:], zt1[:], Vsb[:, :, :, j], ALU.add, [zt1.b] + R8b, [Zn.b])
                    zcur[0] = 1 - zcur[0]

            for seq in range(NSEQ):
                xblocks = [(NMETA + 512 * i, 512, False, i) for i in range(4)]
                if seq == 0:
                    P.memset("dve", Fcar[:], 0.0, [Fcar.b])
                    P.memset("dve", Zs[zcur[0]][:], 0.0, [Zs[zcur[0]].b])
                    P.memset("dve", halo[:], 0.0, halo_b)
                    blocks = [(0, NMETA, True, 0)] + xblocks
                else:
                    P.cp("pool", Fcar[:], snapF[:], [snapF.b], [Fcar.b])
                    P.cp("pool", Zs[zcur[0]][:], snapZ[:], [snapZ.b], [Zs[zcur[0]].b])
                    P.cp("pool", halo[:], snapH[:], [snapH.b], halo_b)
                    blocks = xblocks
                for (pos0, n, is_meta, bi) in blocks:
                    cur_eng[0] = "dve" if is_meta else "pool"
                    tp = min(n, 128)
                    nt = n // tp
                    nch = n // TCH
                    was_pref = (not is_meta) and ((seq, bi) in prefetched)
                    if was_pref:
                        for t in range(4):
                            P.cp("act", xtok[:, t, :], xp[t], [xpb[t]], [xtok.b])
                        qav = QA.h[:, :, :].rearrange("p (j h) n -> p j h n", h=2)
                        P.memset("pool", qav[64:128, :, 0, :], 0.0, [QA.b])
                        P.memset("pool", qav[0:64, :, 1, :], 0.0, [QA.b])
                    else:
                        if is_meta:
                            P.dma("sp", xtok[0:NMETA, 0, :], meta_tokens, writes=[xtok.b])
                        else:
                            P.dma("pool", xtok[:, :, :], x[seq, bi * 512:(bi + 1) * 512, :].rearrange("(t p) d -> p t d", p=128), writes=[xtok.b])
                        rmsnorm_T(xtok, n, tp, nt, gmix, xnT)
                    ckpt("s1_%d_%d" % (seq, pos0))
                    sq, wq = wnext(("in", 0))
                    sk, wk = wnext(("in", 1))
                    for j in range(4):
                        pb = pring.next()
                        for k in range(KT):
                            P.mm(pb.h[:, 0:n], wq[:, k, j * 128:(j + 1) * 128], xnT[:, k, 0:n], k == 0, k == KT - 1, [sq.b, xnT.b], [pb.b])
                        for half in range(2):
                            rows = slice(64 * half, 64 * half + 64)
                            P.act(QA[rows, 2 * j + half, 0:n], pb.h[rows, 0:n], AF.Copy, [pb.b], [QA.b], scale=0.125)
                    for j in range(4):
                        pb = pring.next()
                        for k in range(KT):
                            P.mm(pb.h[:, 0:n], wk[:, k, j * 128:(j + 1) * 128], xnT[:, k, 0:n], k == 0, k == KT - 1, [sk.b, xnT.b], [pb.b])
                        P.cp("dve", KA[:, j, pos0:pos0 + n], pb.h[:, 0:n], [pb.b], [KA.b])
                    pb = pring.next()
                    for k in range(KT):
                        P.mm(pb.h[0:8, 0:n], wf[:, k, :], xnT[:, k, 0:n], k == 0, k == KT - 1, [wf.b, xnT.b], [pb.b])
                    P.act(fe[:, 0:n], pb.h[0:8, 0:n], AF.Exp, [pb.b, negb.b], [fe.b], scale=-1.0, bias=negb[:, 0:1])
                    P.act(fe[:, 0:n], fe[:, 0:n], AF.Ln, [fe.b], [fe.b], bias=1.0)
                    P.op("dve", lambda e, n=n: e.tensor_tensor_scan(out=Fp[:, 0:n], data0=ones[0:8, 0:n], data1=fe[:, 0:n],
                                                                  initial=Fcar[:, 0:1], op0=ALU.mult, op1=ALU.add),
                         [ones.b, fe.b, Fcar.b], [Fp.b])
                    P.cp("dve", Fcar[:, 0:1], Fp[:, n - 1:n], [Fp.b], [Fcar.b])
                    P.ts("dve", QF[0:8, 0:n], Fp[:, 0:n], -1.0, None, ALU.mult, None, [Fp.b], [QF.b])
                    pb = pring.next()
                    for t in range(nt):
                        P.tr(pb.h[0:tp, 8 * t:8 * t + 8], Fp[:, t * 128:t * 128 + tp], identf[0:8, 0:8], [Fp.b, identf.b], [pb.b])
                    kt0 = 0 if is_meta else 1 + 4 * bi
                    P.cp("dve", FT[0:tp, kt0:kt0 + nt, :], pb.h[0:tp, 0:8 * nt].rearrange("p (t h) -> p t h", h=8), [pb.b], [FT.b])
                    sv, wv = wnext(("in", 2))
                    for t in range(nt):
                        pb = pring.next()
                        for k in range(KT):
                            P.mm(pb.h[0:tp, :], xnT[:, k, t * 128:t * 128 + tp], wv[:, k, :], k == 0, k == KT - 1, [sv.b, xnT.b], [pb.b])
                        dstv = Vc[0:tp, kt0 + t, :, :].rearrange("p j (a c) -> p j a c", c=64)[:, :, 0:3:2, :]
                        P.cp("dve", dstv, pb.h[0:tp, :].rearrange("p (j a c) -> p j a c", a=2, c=64), [pb.b], [Vc.b])
                    su, wu = wnext(("in", 3))
                    for ct in range(4):
                        pb = pring.next()
                        for k in range(KT):
                            P.mm(pb.h[:, 0:n], wu[:, k, ct * 128:(ct + 1) * 128], xnT[:, k, 0:n], k == 0, k == KT - 1, [su.b, xnT.b], [pb.b])
                        P.cp("act", UT[:, ct, 0:n], pb.h[:, 0:n], [pb.b], [UT.b])
                    ckpt("s2_%d_%d" % (seq, pos0))
                    vb = [pring.next() for _ in range(4)]
                    vq = [b_.h[:, :].rearrange("p (r c j) -> p r c j", r=2, c=4) for b_ in vb]
                    for ct in range(4):
                        utv = UT[:, ct, 0:n].rearrange("p (j i) -> p j i", i=TCH)
                        for ri in range(2):
                            for ip in range(TCH):
                                for sl in range(4):
                                    rs_ = slice(32 * sl, 32 * sl + 32)
                                    P.mm(vq[sl][:, ri, ct, 0:nch], Gt[rs_, ct, TCH - 1 - ip, ri, :], utv[rs_, :, ip],
                                         ip == 0, ip == TCH - 1, [Gt.b, UT.b], [vb[sl].b], tile_position=(32 * sl, 0))
                    for sl in range(4):
                        dst = Vsb.ap.rearrange("p r (c s) j -> p r c s j", s=4)[:, :, :, sl, 0:nch]
                        P.cp("act", dst, vq[sl][:, :, :, 0:nch], [vb[sl].b], R8b)
                    ckpt("s3_%d_%d" % (seq, pos0))
                    if is_meta:
                        ktiles = [(0, NMETA, True, 0)]
                    else:
                        ktiles = [(0, NMETA, False, 0)] + [(1 + t, 128, False, 0) for t in range(4 * bi)] + \
                                 [(1 + 4 * bi + jj, 128, True, 128 * jj) for jj in range(4)]
                    steps_per_head = (nch + H - 1) // H
                    sdone = [0]
                    obs = {}
                    ptc = [0]

                    def emit_S(h, ti):
                        j, half = h // 2, h % 2
                        rows = slice(64 * half, 64 * half + 64)
                        kt, nk, diag, c0 = ktiles[ti]
                        kp0 = 0 if kt == 0 else NMETA + 128 * (kt - 1)
                        sbk = pring.next()
                        P.mm(sbk.h[0:nk, c0:n], KA[:, j, kp0:kp0 + nk], QA[:, h, c0:n], True, False, [KA.b, QA.b], [sbk.b])
                        P.mm(sbk.h[0:nk, c0:n], selb[:, h, 0:nk], QF[:, c0:n], False, not diag, [selb.b, QF.b], [sbk.b])
                        if diag:
                            P.mm(sbk.h[0:nk, c0:c0 + nk], identb[0:nk, 0:nk], maskb[0:nk, 0:nk], False, True, [identb.b, maskb.b], [sbk.b])
                        pt = PT[ptc[0] % 3]
                        ptc[0] += 1
                        P.act(pt[0:nk, c0:n], sbk.h[0:nk, c0:n], AF.Exp, [sbk.b, FT.b], [pt.b], bias=FT[0:nk, kt, h:h + 1], scale=1.0)
                        return pt

                    def emit_PV(h, ti, pt):
                        j, half = h // 2, h % 2
                        rows = slice(64 * half, 64 * half + 64)
                        kt, nk, diag, c0 = ktiles[ti]
                        if ti == 0:
                            obs[h] = oring.next()
                        ob = obs[h]
                        P.mm(ob.h[:, c0:n], Vc[0:nk, kt, j, 64 * half:64 * half + 128], pt[0:nk, c0:n], ti == 0, ti == len(ktiles) - 1,
                             [Vc.b, pt.b], [ob.b])
                        if ti == len(ktiles) - 1:
                            rc = rec[h % 2]
                            srows = slice(64, 128) if half == 0 else slice(0, 64)
                            P.op("dve", lambda e, rc=rc, ob=ob, rows=rows, srows=srows, n=n: e.reciprocal(out=rc[rows, 0:n], in_=ob.h[srows, 0:n]),
                                 [ob.b], [rc.b])
                            P.tt("dve", attnT[rows, j, 0:n], ob.h[rows, 0:n], rc[rows, 0:n], ALU.mult, [ob.b, rc.b], [attnT.b])
                            s1 = min(nch, sdone[0] + steps_per_head)
                            scan_steps(sdone[0], s1)
                            sdone[0] = s1

                    pend = []
                    for h in range(H):
                        for ti in range(len(ktiles)):
                            pt = emit_S(h, ti)
                            pend.append((h, ti, pt))
                            if len(pend) > 2:
                                emit_PV(*pend.pop(0))
                    while pend:
                        emit_PV(*pend.pop(0))
                    scan_steps(sdone[0], nch)
                    ckpt("s4_%d_%d" % (seq, pos0))
                    for ct in range(4):
                        pb = pring.next()
                        Yv = pb.h[:, 0:n].rearrange("p (j i) -> p j i", i=TCH)
                        utv = UT[:, ct, 0:n].rearrange("p (j i) -> p j i", i=TCH)
                        for tau in range(TCH):
                            P.mm(Yv[:, :, tau:TCH], LagK[:, ct, tau, :], utv[:, :, 0:TCH - tau], tau == 0, False, [LagK.b, UT.b], [pb.b])
                        for i in range(TCH):
                            for sl in range(4):
                                for ri in range(2):
                                    last = (i == TCH - 1 and sl == 3 and ri == 1)
                                    P.mm(Yv[32 * sl:32 * sl + 32, :, i], Hst[:, i, ri, 128 * ct + 32 * sl:128 * ct + 32 * sl + 32],
                                         Sprev[:, ri, 4 * ct + sl, 0:nch], False, last, [Hst.b, Sprev.b], [pb.b], tile_position=(0, 32 * sl))
                        P.act(zT[:, ct, 0:n], pb.h[:, 0:n], AF.Gelu, [pb.b], [zT.b])
                    ckpt("s5_%d_%d" % (seq, pos0))
                    for m2 in range(4):
                        sg_, wg = wnext(("gm", m2))
                        sz_, (wz, wa) = wnext(("zm", m2))
                        for mm_ in range(2):
                            m = 2 * m2 + mm_
                            sset = sig[m % 2]
                            c1 = slice(mm_ * 128, mm_ * 128 + 128)
                            c2 = slice(256 + mm_ * 128, 256 + mm_ * 128 + 128)
                            for which, cs_ in ((0, c1), (1, c2)):
                                pb = pring.next()
                                for k in range(KT):
                                    P.mm(pb.h[:, 0:n], wg[:, k, cs_], xnT[:, k, 0:n], k == 0, k == KT - 1, [sg_.b, xnT.b], [pb.b])
                                P.act(sset[which][:, 0:n], pb.h[:, 0:n], AF.Sigmoid, [pb.b], [sset[which].b])
                            pb = pring.next()
                            for k in range(4):
                                P.mm(pb.h[:, 0:n], wz[:, k, c2], zT[:, k, 0:n], k == 0, k == 3, [sz_.b, zT.b], [pb.b])
                            P.act(sset[2][:, 0:n], pb.h[:, 0:n], AF.Sigmoid, [pb.b], [sset[2].b])
                            pb = pring.next()
                            for k in range(4):
                                P.mm(pb.h[:, 0:n], wz[:, k, c1], zT[:, k, 0:n], k == 0, k == 3, [sz_.b, zT.b], [pb.b])
                            P.tt("dve", ysb[:, 0:n], pb.h[:, 0:n], sset[2][:, 0:n], ALU.mult, [pb.b, sset[2].b], [ysb.b])
                            P.tt(cur_eng[0], t2b[:, 0:n], ysb[:, 0:n], sset[1][:, 0:n], ALU.mult, [ysb.b, sset[1].b], [t2b.b])
                            pb = pring.next()
                            for k in range(4):
                                P.mm(pb.h[:, 0:n], wa[:, k, c1], attnT[:, k, 0:n], k == 0, k == 3, [sz_.b, attnT.b], [pb.b])
                            P.tt("dve", t1b[:, 0:n], pb.h[:, 0:n], sset[0][:, 0:n], ALU.mult, [pb.b, sset[0].b], [t1b.b])
                            P.tt(cur_eng[0], mergedT[:, m, 0:n], t1b[:, 0:n], t2b[:, 0:n], ALU.add, [t1b.b, t2b.b], R8b)
                    for hh in range(2):
                        so, wo = wnext(("o", hh))
                        for t in range(nt):
                            pb = pring.next()
                            for k in range(KT):
                                P.mm(pb.h[0:tp, :], mergedT[:, k, t * 128:t * 128 + tp], wo[:, k, :], k == 0, k == KT - 1, [so.b] + R8b, [pb.b])
                            dsth = xtok[0:tp, t, hh * 512:(hh + 1) * 512]
                            P.tt("dve", dsth, dsth, pb.h[0:tp, :], ALU.add, [pb.b, xtok.b], [xtok.b])
                    ckpt("s6_%d_%d" % (seq, pos0))
                    rmsnorm_T(xtok, n, tp, nt, gffn, xn2T)

                    def ffn_down(g, n=n, tp=tp, nt=nt):
                        nu = 2 if g < 5 else 1
                        at = actT[g % 2]
                        sd, wd = wnext(("dn", g))
                        for t in range(nt):
                            for hh in range(2):
                                pb = pring.next()
                                for il in range(2 * nu):
                                    P.mm(pb.h[0:tp, :], at[:, il, t * 128:t * 128 + tp], wd[:, il, hh * 512:(hh + 1) * 512],
                                         il == 0, il == 2 * nu - 1, [at.b, sd.b], [pb.b])
                                dsth = xtok[0:tp, t, hh * 512:(hh + 1) * 512]
                                P.tt("dve", dsth, dsth, pb.h[0:tp, :], ALU.add, [pb.b, xtok.b], [xtok.b])

                    nxt = None
                    if not is_meta:
                        k_ = xorder.index((seq, bi))
                        if k_ + 1 < len(xorder):
                            nxt = xorder[k_ + 1]
                    if nxt is not None:
                        for t in range(4):
                            P.dma("pool", xp[t], x[nxt[0], nxt[1] * 512 + t * 128:nxt[1] * 512 + (t + 1) * 128, :], writes=[xpb[t]])
                    pend_conv = []
                    for g in range(6):
                        nunits = 2 if g < 5 else 1
                        at = actT[g % 2]
                        if nxt is not None and g == 2:
                            rms_stats(xp, xpb, ss2, rstd2, xnb[1][:, :], [xnb[1].b], 128, 4)
                        for uu in range(nunits):
                            c = 2 * g + uu
                            sup, wup = wnext(("up", c))
                            for e_ in range(2):
                                i = 2 * c + e_
                                il = 2 * uu + e_
                                parts = ((0, slice(e_ * 128, e_ * 128 + 128), i, cv[i % 2]),
                                         (1, slice(256 + e_ * 128, 256 + e_ * 128 + 128), NFT + i, cg[i % 2]))
                                for which, cs_, ci, dst in parts:
                                    pb = pring.next()
                                    for k in range(KT):
                                        P.mm(pb.h[:, 0:n], wup[:, k, cs_], xn2T[:, k, 0:n], k == 0, k == KT - 1, [sup.b, xn2T.b], [pb.b])
                                    hbuf = hb[which][i % 2]
                                    P.cp(cur_eng[0], hbuf[:, 0:2], halo[:, ci, :], [halo_b[ci]], [hbuf.b])
                                    P.cp("dve", hbuf[:, 2:2 + n], pb.h[:, 0:n], [pb.b], [hbuf.b])
                                    P.cp(cur_eng[0], halo[:, ci, :], hbuf[:, n:n + 2], [hbuf.b], [halo_b[ci]])
                                    if not is_meta:
                                        dgs = dg[i % 2][which]
                                        for tap in range(3):
                                            P.ts("dve", dgs[tap][:], identb[:], cwb[:, ci, tap:tap + 1], None, ALU.mult, None,
                                                 [identb.b, cwb.b], [dgs[tap].b])

                                        def conv_part(which=which, ci=ci, dst=dst, hbuf=hbuf, dgs=dgs, i=i, il=il, at=at, n=n):
                                            pc = pring.next()
                                            for tap in range(3):
                                                P.mm(pc.h[:, 0:n], dgs[tap][:], hbuf[:, tap:tap + n], tap == 0, tap == 2,
                                                     [dgs[tap].b, hbuf.b], [pc.b])
                                            if which == 0:
                                                P.act(dst[:, 0:n], pc.h[:, 0:n], AF.Identity, [pc.b, cwb.b], [dst.b], bias=cwb[:, ci, 3:4])
                                            else:
                                                P.act(sgb[:, 0:n], pc.h[:, 0:n], AF.Silu, [pc.b, cwb.b], [sgb.b], bias=cwb[:, ci, 3:4])
                                                P.tt("pool", at[:, il, 0:n], sgb[:, 0:n], cv[i % 2][:, 0:n], ALU.mult, [sgb.b, cv[i % 2].b], [at.b])

                                        pend_conv.append(conv_part)
                                        if len(pend_conv) > 1:
                                            pend_conv.pop(0)()
                            if not is_meta and uu == 0 and g >= 1:
                                ffn_down(g - 1)
                        if not is_meta and g == 5:
                            while pend_conv:
                                pend_conv.pop(0)()
                            ffn_down(5)
                            if nxt is not None:
                                rms_T(xp, xpb, rstd2, 128, 4, gmix, xnT)
                                prefetched.add(nxt)
                    ckpt("s7_%d_%d" % (seq, pos0))
                    if is_meta:
                        P.cp("dve", snapF[:], Fcar[:], [Fcar.b], [snapF.b])
                        P.cp("dve", snapZ[:], Zs[zcur[0]][:], [Zs[zcur[0]].b], [snapZ.b])
                        P.cp("dve", snapH[:], halo[:], halo_b, [snapH.b])
                    if not is_meta:
                        P.memset("pool", ss[:], 0.0, [ss.b])
                        for t in range(nt):
                            P.act(junk[0:tp, :], xtok[0:tp, t, :], AF.Square, [xtok.b, ss.b], [junk.b, ss.b], accum_out=ss[0:tp, t:t + 1])
                        P.act(rstd[0:tp, 0:nt], ss[0:tp, 0:nt], AF.Sqrt, [ss.b], [rstd.b], scale=1.0 / D, bias=EPS)
                        P.op("dve", lambda e: e.reciprocal(out=rstd[0:128, 0:4], in_=rstd[0:128, 0:4]), [rstd.b], [rstd.b])
                        for t in range(nt):
                            P.stt("dve", xtok[0:tp, t, :], xtok[0:tp, t, :], rstd[0:tp, t:t + 1], gfin[0:tp, :], ALU.mult, ALU.mult,
                                  [xtok.b, rstd.b, gfin.b], [xtok.b])
                        P.dma("pool", y[seq, bi * 512:(bi + 1) * 512, :].rearrange("(t p) d -> p t d", p=128), xtok[:, :, :], reads=[xtok.b], writes=[Buf()])

        except _Stop:
            pass
        SBUF_LEFT.append(nc.sbuf_bytes_remaining)
        P.barrier()
        P.emit()
    return nc


_CACHE = {}


def _consts():
    ident = np.eye(128, dtype=np.float32)
    k = np.arange(128)[:, None]
    q = np.arange(128)[None, :]
    mask = np.where(k <= q, 0.0, MASKV).astype(np.float32)
    sel = np.zeros((8, 8, 128), dtype=np.float32)
    for h in range(8):
        sel[h, h, :] = 1.0
    bd = (k // 16 == q // 16).astype(np.float32)
    return {"c_ident": ident, "c_mask": mask, "c_sel": sel, "c_bd": bd}


def kernel(**inputs):
    if "nc" not in _CACHE:
        _CACHE["nc"] = build_program()
    nc = _CACHE["nc"]
    consts = _consts()
    ncores = 8
    in_maps = []
    for c in range(ncores):
        m = {}
        for k_, v in inputs.items():
            a = np.ascontiguousarray(np.asarray(v, dtype=np.float32))
            if k_ == "x":
                a = np.ascontiguousarray(a[NSEQ * c:NSEQ * (c + 1)])
            m[k_] = a
        m.update(consts)
        in_maps.append(m)
    res = run_bass_kernel_spmd(nc, in_maps, core_ids=list(range(ncores)))
    out = np.concatenate([np.asarray(r["y"], dtype=np.float32) for r in res.results], axis=0)
    return out
```

```python
import math
import os
import contextlib
DBG = os.environ.get('DBG', '')
import numpy as np
import concourse.bass as bass
import concourse.mybir as mybir
from concourse.bass_utils import run_bass_kernel_spmd

F32 = mybir.dt.float32
BF16 = mybir.dt.bfloat16
I32 = mybir.dt.int32
AF = mybir.ActivationFunctionType
ALU = mybir.AluOpType

D = 1024
KT = 8
H = 8
NMETA = 16
SEQ = 2048
L = NMETA + SEQ
DFF = 2816
NFT = 22
TCH = 8
NG = 32
NSTRIP = 16
EPS = 1e-6
NSEQ = 2
IN_W = 4104
MASKV = -30000.0


class Buf:
    __slots__ = ("writers", "readers", "excl")

    def __init__(self, excl=False):
        self.writers = {}
        self.readers = {}
        self.excl = excl


class Prog:
    ENG = ("pe", "act", "dve", "pool", "sp")

    def __init__(self, nc, n_streams=16):
        self.nc = nc
        self.ops = {e: [] for e in self.ENG}
        self.cnt = {e: 0 for e in self.ENG}
        self.n_streams = n_streams
        self.stream_cnt = [0] * n_streams
        self.waited = {e: {} for e in self.ENG}
        self.groups = {"sp": list(range(0, 8)), "pool": list(range(8, 12)), "cast": list(range(12, 16))}
        self.rr = {"sp": 0, "pool": 0, "cast": 0}

    def _deps(self, eng, reads, writes, skip_pe=False):
        deps = {}

        def add(d):
            if d is None:
                return
            k, v = d
            if deps.get(k, 0) < v:
                deps[k] = v
        for b in reads:
            for d in b.writers.items():
                add(d)
        for b in writes:
            for d in b.writers.items():
                add(d)
            for d in b.readers.items():
                add(d)
        waits = []
        w = self.waited[eng]
        for k, v in deps.items():
            if skip_pe and k == "pe":
                continue
            if w.get(k, 0) >= v:
                continue
            w[k] = v
            waits.append((k, v))
        return waits

    def _commit(self, me, reads, writes):
        k, v = me
        for b in reads:
            if b.readers.get(k, 0) < v:
                b.readers[k] = v
        for b in writes:
            if b.writers.get(k, 0) < v:
                b.writers[k] = v

    def op(self, eng, fn, reads=(), writes=(), pe_mm=False):
        ex = [b for b in reads if b.excl]
        if ex:
            writes = list(writes) + ex
        waits = self._deps(eng, reads, writes, skip_pe=pe_mm)
        self.cnt[eng] += 1
        me = (eng, self.cnt[eng])
        self.ops[eng].append((waits, fn, ("c", eng)))
        self._commit(me, reads, writes)
        return me

    def dma(self, eng, out, in_, reads=(), writes=(), cast=False, **kw):
        grp = "cast" if cast else eng
        lst = self.groups[grp]
        stream = lst[self.rr[grp]]
        self.rr[grp] = (self.rr[grp] + 1) % len(lst)
        key = ("d", stream)
        waits = self._deps(eng, reads, writes)
        prev = self.stream_cnt[stream] * 16
        if prev and self.waited[eng].get(key, 0) < prev:
            self.waited[eng][key] = prev
            waits.append((key, prev))
        self.stream_cnt[stream] += 1
        me = (key, self.stream_cnt[stream] * 16)
        self.ops[eng].append((waits, lambda e, o=out, i=in_, kw=kw: e.dma_start(out=o, in_=i, **kw), ("d", stream)))
        self._commit(me, reads, writes)
        return me

    def barrier(self, skip_cast=False):
        targets = [(e, self.cnt[e]) for e in self.ENG if self.cnt[e]]
        targets += [(("d", s), self.stream_cnt[s] * 16) for s in range(self.n_streams)
                    if self.stream_cnt[s] and not (skip_cast and s in self.groups["cast"])]
        for e in self.ENG:
            waits = []
            for k, v in targets:
                if self.waited[e].get(k, 0) < v:
                    self.waited[e][k] = v
                    waits.append((k, v))
            if waits:
                self.ops[e].append((waits, None, None))

    def act(self, out, in_, func, reads, writes, **kw):
        return self.op("act", lambda e: e.activation(out=out, in_=in_, func=func, **kw), reads, writes)

    def tt(self, eng, out, in0, in1, op, reads, writes):
        return self.op(eng, lambda e: e.tensor_tensor(out=out, in0=in0, in1=in1, op=op), reads, writes)

    def ts(self, eng, out, in0, s1, s2, op0, op1, reads, writes):
        if s2 is None:
            return self.op(eng, lambda e: e.tensor_scalar(out=out, in0=in0, scalar1=s1, scalar2=None, op0=op0), reads, writes)
        return self.op(eng, lambda e: e.tensor_scalar(out=out, in0=in0, scalar1=s1, scalar2=s2, op0=op0, op1=op1), reads, writes)

    def stt(self, eng, out, in0, scalar, in1, op0, op1, reads, writes):
        return self.op(eng, lambda e: e.scalar_tensor_tensor(out=out, in0=in0, scalar=scalar, in1=in1, op0=op0, op1=op1), reads, writes)

    def cp(self, eng, out, in_, reads, writes):
        if eng == "act":
            return self.op("act", lambda e: e.activation(out=out, in_=in_, func=AF.Copy), reads, writes)
        return self.op(eng, lambda e: e.tensor_copy(out=out, in_=in_), reads, writes)

    def memset(self, eng, ap, val, writes):
        return self.op(eng, lambda e: e.memset(ap, val), (), writes)

    def mm(self, out, lhsT, rhs, start, stop, reads, writes, tile_position=None):
        if tile_position is None:
            fn = lambda e: e.matmul(out, lhsT=lhsT, rhs=rhs, start=start, stop=stop)
        else:
            fn = lambda e: e.matmul(out, lhsT=lhsT, rhs=rhs, start=start, stop=stop, tile_position=tile_position)
        return self.op("pe", fn, reads, writes, pe_mm=True)

    def tr(self, out, in_, ident, reads, writes):
        return self.op("pe", lambda e: e.transpose(out=out, in_=in_, identity=ident), reads, writes, pe_mm=True)

    def emit(self):
        nc = self.nc
        with contextlib.ExitStack() as st:
            sems = {}
            for e in self.ENG:
                sems[e] = st.enter_context(nc.semaphore("s_" + e))
            for i in range(self.n_streams):
                sems[("d", i)] = st.enter_context(nc.semaphore("d%d" % i))
            block = st.enter_context(nc.Block())
            engmap = {"pe": "tensor", "act": "scalar", "dve": "vector", "pool": "gpsimd", "sp": "sync"}

            def make(ename):
                oplist = self.ops[ename]

                def body(eng):
                    for waits, fn, inc in oplist:
                        for k, v in waits:
                            eng.wait_ge(sems[k], v)
                        if fn is None:
                            continue
                        ins = fn(eng)
                        if inc[0] == "c":
                            ins.then_inc(sems[inc[1]], 1)
                        else:
                            ins.then_inc(sems[inc], 16)
                return body

            for e in self.ENG:
                if self.ops[e]:
                    getattr(block, engmap[e])(make(e))


class TB:
    def __init__(self, h):
        self.h = h
        self.b = Buf()

    def __getitem__(self, k):
        return self.h[k]


class Ring:
    def __init__(self, items):
        self.items = items
        self.i = 0

    def next(self):
        it = self.items[self.i]
        self.i = (self.i + 1) % len(self.items)
        return it


class _Stop(Exception):
    pass


CKPTS = []
SBUF_LEFT = []


def build_program(debug=False, limit=None):
    nc = bass.Bass("TRN2", target_bir_lowering=False)
    dr = lambda name, shape, dt=F32, kind="ExternalInput": nc.dram_tensor(name, list(shape), dt, kind=kind).ap()
    x = dr("x", [NSEQ, SEQ, D])
    meta_tokens = dr("meta_tokens", [NMETA, D])
    norm_mix_g = dr("norm_mix_g", [1, D])
    w_in = dr("w_in", [1, D, IN_W])
    b_forget = dr("b_forget", [1, H])
    w_attn_out = dr("w_attn_out", [1, 512, D])
    lam_re = dr("ssm_lambda_re", [1, NG, 64])
    lam_im = dr("ssm_lambda_im", [1, NG, 64])
    b_re = dr("ssm_b_re", [1, NG, 64, 16])
    b_im = dr("ssm_b_im", [1, NG, 64, 16])
    c_re = dr("ssm_c_re", [1, NG, 16, 64])
    c_im = dr("ssm_c_im", [1, NG, 16, 64])
    ssm_d = dr("ssm_d", [1, 512])
    log_dt = dr("ssm_log_dt", [1, NG])
    w_glu = dr("w_glu", [1, 512, 2 * D])
    w_o = dr("w_o", [1, D, D])
    norm_ffn_g = dr("norm_ffn_g", [1, D])
    w_up = dr("w_ffn_up", [1, D, 2 * DFF])
    conv_w = dr("ffn_conv_w", [1, 3, 2 * DFF])
    conv_b = dr("ffn_conv_b", [1, 2 * DFF])
    w_down = dr("w_ffn_down", [1, DFF, D])
    norm_final_g = dr("norm_final_g", [D])
    c_ident = dr("c_ident", [128, 128])
    c_mask = dr("c_mask", [128, 128])
    c_sel = dr("c_sel", [8, 8, 128])
    c_bd = dr("c_bd", [128, 128])
    y = dr("y", [NSEQ, SEQ, D], kind="ExternalOutput")
    ws_in = dr("ws_in", [4, 128, 8, 512], BF16, "Internal")
    ws_f = dr("ws_f", [128, 8, 8], BF16, "Internal")
    ws_gm = dr("ws_gm", [4, 128, 8, 512], BF16, "Internal")
    ws_zm = dr("ws_zm", [4, 128, 4, 512], BF16, "Internal")
    ws_am = dr("ws_am", [4, 128, 4, 256], BF16, "Internal")
    ws_o = dr("ws_o", [2, 128, 8, 512], BF16, "Internal")
    ws_up = dr("ws_up", [11, 128, 8, 512], BF16, "Internal")
    ws_dn = dr("ws_dn", [6, 128, 4, 1024], BF16, "Internal")

    st = contextlib.ExitStack()
    with st:
        def sb(name, shape, dt):
            return TB(st.enter_context(nc.sbuf_tensor(name, list(shape), dt)))

        def psb(name, shape, dt):
            t = TB(st.enter_context(nc.psum_tensor(name, list(shape), dt)))
            t.b.excl = True
            return t

        P = Prog(nc)

        def ckpt(name):
            CKPTS.append((name, dict(P.cnt)))
            if limit is not None and name == limit:
                if DBG.startswith('pad'):
                    eng_, n_ = DBG[3:].split(':')
                    for _ in range(int(n_)):
                        if eng_ == 'pe':
                            P.mm(pbanks[0].h[:, 0:128], identb[:, :], identb[:, :], True, True, [identb.b], [pbanks[0].b])
                        else:
                            P.memset(eng_, zt1[:], 0.0, [zt1.b])
                raise _Stop()

        identf = sb("identf", [128, 128], F32)
        identb = sb("identb", [128, 128], BF16)
        maskb = sb("maskb", [128, 128], BF16)
        selb = sb("selb", [128, 8, 128], BF16)
        bdm = sb("bdm", [128, 128], F32)
        ones = sb("ones", [128, 512], F32)
        gmix = sb("gmix", [128, 8], F32)
        gffn = sb("gffn", [128, 8], F32)
        gfin = sb("gfin", [128, D], F32)
        dcol = sb("dcol", [128, 4], F32)
        negb = sb("negb", [8, 1], F32)
        cwb = sb("cwb", [128, 44, 4], F32)
        wf = sb("wf", [128, 8, 8], BF16)
        KA = sb("KA", [128, 4, L], BF16)
        Vc = sb("Vc", [128, 17, 4, 192], BF16)
        FT = sb("FT", [128, 17, 8], F32)
        LagK = sb("LagK", [128, 4, TCH, 128], BF16)
        Gt = sb("Gt", [128, 4, TCH, 2, 128], BF16)
        Hst = sb("Hst", [128, TCH, 2, 512], BF16)
        CA = sb("CA", [128, 2, NSTRIP], F32)
        CB = sb("CB", [128, 2, NSTRIP], F32)
        xtok = sb("xtok", [128, 4, D], F32)
        xnb = [sb("xnb%d" % i, [128, D], BF16) for i in range(2)]
        ss = sb("ss", [128, 4], F32)
        ss2 = sb("ss2", [128, 4], F32)
        rstd2 = sb("rstd2", [128, 4], F32)
        rstd = sb("rstd", [128, 4], F32)
        xnT = sb("xnT", [128, 8, 512], BF16)
        xn2T = xnT
        QA = sb("QA", [128, 8, 512], BF16)
        QF = sb("QF", [128, 512], BF16)
        Fcar = sb("Fcar", [8, 1], F32)
        UT = sb("UT", [128, 4, 512], BF16)
        attnT = sb("attnT", [128, 4, 512], BF16)
        zT = sb("zT", [128, 4, 512], BF16)
        R8 = sb("R8", [128, 4096], BF16)
        R8lo = Buf()
        R8hi = Buf()
        R8b = [R8lo, R8hi]

        class View:
            def __init__(self, ap, b):
                self.ap = ap
                self.b = b

            def __getitem__(self, k):
                return self.ap[k]

        junk = View(R8.h[:, 0:1024], R8lo)
        mergedT = View(R8.h[:, :].rearrange("p (m n) -> p m n", n=512), None)
        Vsb = View(R8.h[:, :].bitcast(F32).rearrange("p (r s j) -> p r s j", r=2, s=NSTRIP), None)
        actT = [View(R8.h[:, 2048 * i:2048 * i + 2048].rearrange("p (m n) -> p m n", n=512), R8b[i]) for i in range(2)]
        PT = [sb("PT%d" % i, [128, 512], BF16) for i in range(3)]
        Sprev = sb("Sprev", [128, 2, NSTRIP, 64], BF16)
        Zs = [sb("Zs%d" % i, [128, 2, NSTRIP], F32) for i in range(2)]
        zt1 = sb("zt1", [128, 2, NSTRIP], F32)
        zt2 = sb("zt2", [128, 2, NSTRIP], F32)
        sig = [[sb("sig%d_%d" % (i, j), [128, 512], BF16) for j in range(3)] for i in range(2)]
        ft = [sb("ft%d" % i, [128, 512], F32) for i in range(5)]
        ysb, t1b, t2b = ft[0], ft[1], ft[2]
        cv = [ft[0], ft[1]]
        cg = [ft[2], ft[3]]
        sgb = ft[4]
        rec = [ft[3], ft[4]]
        fe = View(ft[0].h[0:8, :], ft[0].b)
        Fp = View(ft[1].h[0:8, :], ft[1].b)
        halo = sb("halo", [128, 44, 2], F32)
        halo_b = [Buf() for _ in range(44)]
        snapF = sb("snapF", [8, 1], F32)
        snapZ = sb("snapZ", [128, 2, NSTRIP], F32)
        snapH = sb("snapH", [128, 44, 2], F32)
        hb = [[sb("hb%d_%d" % (w_, i_), [128, 516], BF16) for i_ in range(2)] for w_ in range(2)]
        dg = [[[sb("dg%d_%d_%d" % (p_, w_, t_), [128, 128], BF16) for t_ in range(3)] for w_ in range(2)] for p_ in range(2)]
        NSLOT = 3
        slots = [sb("slot%d" % i, [128, 4096], BF16) for i in range(NSLOT)]
        slot_ring = Ring(slots)
        pbanks = [psb("pb%d" % i, [128, 512], F32) for i in range(6)]
        pring = Ring(pbanks)
        obanks = [psb("ob%d" % i, [128, 512], F32) for i in range(2)]
        oring = Ring(obanks)

        def bank_bf16(pb):
            return pb.h[:].bitcast(BF16)

        kp = lambda ap: ap.rearrange("(k p) j -> p k j", p=128)
        GB = 2056
        cast_bufs = {}

        def ensure_cast(name, c):
            key = (name, c)
            if key in cast_bufs:
                return cast_bufs[key]
            b = Buf()
            cast_bufs[key] = [b]
            cd = lambda o, i: P.dma("pool", o, i, writes=[b], cast=True)
            if name == "in":
                lo = [0, 512, 1024, 1544][c]
                cd(ws_in[c], kp(w_in[0][:, lo:lo + 512]))
            elif name == "f":
                cd(ws_f, kp(w_in[0][:, 1536:1544]))
            elif name == "gm":
                cd(ws_gm[c][:, :, 0:256], kp(w_in[0][:, GB + 256 * c: GB + 256 * c + 256]))
                cd(ws_gm[c][:, :, 256:512], kp(w_in[0][:, GB + 1024 + 256 * c: GB + 1024 + 256 * c + 256]))
            elif name == "zm":
                cd(ws_zm[c][:, :, 0:256], kp(w_glu[0][:, 256 * c: 256 * c + 256]))
                cd(ws_zm[c][:, :, 256:512], kp(w_glu[0][:, 1024 + 256 * c: 1024 + 256 * c + 256]))
            elif name == "am":
                cd(ws_am[c], kp(w_attn_out[0][:, 256 * c: 256 * c + 256]))
            elif name == "o":
                cd(ws_o[c], kp(w_o[0][:, 512 * c: 512 * c + 512]))
            elif name == "up":
                cd(ws_up[c][:, :, 0:256], kp(w_up[0][:, 256 * c: 256 * c + 256]))
                cd(ws_up[c][:, :, 256:512], kp(w_up[0][:, DFF + 256 * c: DFF + 256 * c + 256]))
            elif name == "dn":
                nt_ = 4 if c < 5 else 2
                cd(ws_dn[c][:, 0:nt_, :], w_down[0][512 * c: 512 * c + 128 * nt_, :].rearrange("(t p) j -> p t j", p=128))
            return cast_bufs[key]

        tmpb = Buf()

        def xt_tmp(i):
            return xtok.h[:, i // 2, (i % 2) * 512:(i % 2) * 512 + 512]

        def slot_tmp(i):
            return slots[i // 4].h[:, (i % 4) * 1024:(i % 4) * 1024 + 1024].bitcast(F32)

        class Tmp:
            def __init__(self, ap):
                self.ap = ap
                self.b = Buf()

            def g(self):
                return self.ap.rearrange("p (g c) -> p g c", c=16)

            def s(self):
                return self.ap[:, 0:32]

        def xn_tmp(i):
            return xnT.h[:, 2 * i:2 * i + 2, :].rearrange("p a b -> p (a b)").bitcast(F32)

        tmps = [Tmp(xt_tmp(i)) for i in range(8)] + [Tmp(slot_tmp(i)) for i in range(4 * NSLOT)] + [Tmp(xn_tmp(i)) for i in range(4)]
        free_tmps = list(tmps)

        def newtmp():
            return free_tmps.pop(0)

        def rel(*ts_):
            for t in ts_:
                free_tmps.append(t)

        try:
            P.dma("sp", identf[:], c_ident, writes=[identf.b])
            P.cp("dve", identb[:], identf[:], [identf.b], [identb.b])
            t_m = newtmp()
            P.dma("sp", t_m.ap[:, 0:128], c_mask, writes=[t_m.b])
            P.cp("dve", maskb[:], t_m.ap[:, 0:128], [t_m.b], [maskb.b])
            csel2 = c_sel.rearrange("r h k -> r (h k)")
            P.memset("dve", selb[:], 0.0, [selb.b])
            P.memset("dve", QA[:], 0.0, [QA.b])
            P.memset("dve", QF[:], 0.0, [QF.b])
            for hh in range(2):
                P.dma("sp", t_m.ap[0:8, :], csel2[:, 512 * hh:512 * hh + 512], reads=[], writes=[t_m.b])
                P.cp("dve", selb[0:8, 4 * hh:4 * hh + 4, :], t_m.ap[0:8, :].rearrange("p (h k) -> p h k", k=128), [t_m.b], [selb.b])
            P.dma("sp", bdm[:], c_bd, writes=[bdm.b])
            P.memset("pool", ones[:], 1.0, [ones.b])
            P.dma("sp", gfin[:], norm_final_g.partition_broadcast(128), writes=[gfin.b])
            P.dma("sp", wf[:], ws_f, reads=ensure_cast("f", 0), writes=[wf.b])
            P.memset("pool", Vc[:], 1.0, [Vc.b])
            P.memset("pool", FT[:], 0.0, [FT.b])

            for dst, src, r in ((gmix, norm_mix_g, 8), (gffn, norm_ffn_g, 8), (dcol, ssm_d, 4)):
                P.dma("sp", t_m.ap[0:r, 0:128], src[0].rearrange("(k p) -> k p", p=128), writes=[t_m.b])
                pb = pring.next()
                P.tr(pb.h[:, 0:r], t_m.ap[0:r, 0:128], identf[0:r, 0:r], [t_m.b, identf.b], [pb.b])
                P.cp("dve", dst[:], pb.h[:, 0:r], [pb.b], [dst.b])
            for part in range(11):
                P.dma("sp", t_m.ap[0:3, :], conv_w[0][:, part * 512:(part + 1) * 512], writes=[t_m.b])
                P.dma("sp", t_m.ap[3:4, :], conv_b[0][part * 512:(part + 1) * 512].rearrange("(o n) -> o n", o=1), writes=[t_m.b])
                pb = pring.next()
                for j in range(4):
                    P.tr(pb.h[:, 4 * j:4 * j + 4], t_m.ap[0:4, j * 128:(j + 1) * 128], identf[0:4, 0:4], [t_m.b, identf.b], [pb.b])
                P.cp("dve", cwb[:, 4 * part:4 * part + 4, :], pb.h[:, 0:16].rearrange("p (i j) -> p i j", j=4), [pb.b], [cwb.b])
            P.dma("sp", negb[:], b_forget[0].rearrange("(h o) -> h o", o=1), writes=[negb.b])
            P.ts("dve", negb[:], negb[:], -1.0, None, ALU.mult, None, [negb.b], [negb.b])

            def dup_T(src):
                P.dma("sp", t_m.ap[0:32, 0:64], src, writes=[t_m.b])
                P.dma("sp", t_m.ap[0:32, 64:128], src, writes=[t_m.b])
                pb = pring.next()
                P.tr(pb.h[:, 0:32], t_m.ap[0:32, 0:128], identf[0:32, 0:32], [t_m.b, identf.b], [pb.b])
                o = newtmp()
                P.cp("dve", o.s(), pb.h[:, 0:32], [pb.b], [o.b])
                return o

            LR = dup_T(lam_re[0])
            LI = dup_T(lam_im[0])
            DT = newtmp()
            P.dma("sp", DT.s(), log_dt[0].partition_broadcast(128), writes=[DT.b])
            P.act(DT.s(), DT.s(), AF.Exp, [DT.b], [DT.b])
            PM = newtmp()
            P.memset("pool", PM.s(), 0.0, [PM.b])
            P.memset("pool", PM.ap[0:64, 0:32].rearrange("p (s two) -> p s two", two=2)[:, :, 0], 1.0, [PM.b])
            P.memset("pool", PM.ap[64:128, 0:32].rearrange("p (s two) -> p s two", two=2)[:, :, 1], 1.0, [PM.b])
            for _c in range(4):
                ensure_cast("in", _c)
            for _c in range(4):
                ensure_cast("gm", _c)
                ensure_cast("zm", _c)
                ensure_cast("am", _c)
            for _c in range(2):
                ensure_cast("o", _c)
            for _g in range(6):
                ensure_cast("up", 2 * _g)
                if _g < 5:
                    ensure_cast("up", 2 * _g + 1)
                ensure_cast("dn", _g)

            TWO_PI = 2.0 * math.pi

            def sin_of(dst, src, shift, scratch_f, scratch_i):
                a = dst.s()
                kf = scratch_f.s()
                ki = scratch_i.ap[:, 0:32].bitcast(I32)
                P.ts("dve", a, src.s(), 1.0, shift, ALU.mult, ALU.add, [src.b], [dst.b])
                P.ts("dve", kf, a, 1.0 / TWO_PI, None, ALU.mult, None, [dst.b], [scratch_f.b])
                P.cp("dve", ki, kf, [scratch_f.b], [scratch_i.b])
                P.cp("dve", kf, ki, [scratch_i.b], [scratch_f.b])
                P.stt("dve", a, kf, -TWO_PI, a, ALU.mult, ALU.add, [scratch_f.b, dst.b], [dst.b])
                P.ts("dve", kf, a, -math.pi, None, ALU.is_lt, None, [dst.b], [scratch_f.b])
                P.stt("dve", a, kf, TWO_PI, a, ALU.mult, ALU.add, [scratch_f.b, dst.b], [dst.b])
                P.ts("dve", kf, a, math.pi, None, ALU.is_gt, None, [dst.b], [scratch_f.b])
                P.stt("dve", a, kf, -TWO_PI, a, ALU.mult, ALU.add, [scratch_f.b, dst.b], [dst.b])
                P.ts("dve", a, a, -3.141592, 3.141592, ALU.max, ALU.min, [dst.b], [dst.b])
                P.act(a, a, AF.Sin, [dst.b], [dst.b])

            AR = newtmp()
            AI = newtmp()
            P.tt("dve", AR.s(), LR.s(), DT.s(), ALU.mult, [LR.b, DT.b], [AR.b])
            P.tt("dve", AI.s(), LI.s(), DT.s(), ALU.mult, [LI.b, DT.b], [AI.b])
            MAG = newtmp()
            P.act(MAG.s(), AR.s(), AF.Exp, [AR.b], [MAG.b])
            SN = newtmp()
            CS = newtmp()
            SC1 = newtmp()
            SC2 = newtmp()
            sin_of(SN, AI, 0.0, SC1, SC2)
            sin_of(CS, AI, math.pi / 2, SC1, SC2)
            LBr = AR
            LBi = AI
            P.tt("dve", LBr.s(), MAG.s(), CS.s(), ALU.mult, [MAG.b, CS.b], [LBr.b])
            P.tt("dve", LBi.s(), MAG.s(), SN.s(), ALU.mult, [MAG.b, SN.b], [LBi.b])
            NR = MAG
            P.ts("dve", NR.s(), LBr.s(), -1.0, None, ALU.add, None, [LBr.b], [NR.b])
            DEN = SN
            P.tt("dve", DEN.s(), LR.s(), LR.s(), ALU.mult, [LR.b], [DEN.b])
            P.tt("dve", SC1.s(), LI.s(), LI.s(), ALU.mult, [LI.b], [SC1.b])
            P.tt("dve", DEN.s(), DEN.s(), SC1.s(), ALU.add, [DEN.b, SC1.b], [DEN.b])
            P.op("dve", lambda e: e.reciprocal(out=DEN.s(), in_=DEN.s()), [DEN.b], [DEN.b])
            KR = CS
            KI = SC2
            P.tt("dve", KR.s(), NR.s(), LR.s(), ALU.mult, [NR.b, LR.b], [KR.b])
            P.tt("dve", SC1.s(), LBi.s(), LI.s(), ALU.mult, [LBi.b, LI.b], [SC1.b])
            P.tt("dve", KR.s(), KR.s(), SC1.s(), ALU.add, [KR.b, SC1.b], [KR.b])
            P.tt("dve", KR.s(), KR.s(), DEN.s(), ALU.mult, [KR.b, DEN.b], [KR.b])
            P.tt("dve", KI.s(), LBi.s(), LR.s(), ALU.mult, [LBi.b, LR.b], [KI.b])
            P.tt("dve", SC1.s(), NR.s(), LI.s(), ALU.mult, [NR.b, LI.b], [SC1.b])
            P.tt("dve", KI.s(), KI.s(), SC1.s(), ALU.subtract, [KI.b, SC1.b], [KI.b])
            P.tt("dve", KI.s(), KI.s(), DEN.s(), ALU.mult, [KI.b, DEN.b], [KI.b])
            P.tt("dve", KR.s(), KR.s(), PM.s(), ALU.mult, [KR.b, PM.b], [KR.b])
            P.tt("dve", KI.s(), KI.s(), PM.s(), ALU.mult, [KI.b, PM.b], [KI.b])
            rel(LR, LI, DT, MAG, SN, SC1)

            def bc(t):
                return t.s().unsqueeze(2).to_broadcast([128, 32, 16])

            Br = newtmp()
            Bi = newtmp()
            for dst, src in ((Br, b_re), (Bi, b_im)):
                for half in range(2):
                    for q4 in range(4):
                        P.dma("sp", dst.g()[64 * half:64 * half + 64, 8 * q4:8 * q4 + 8, :],
                              src[0][8 * q4:8 * q4 + 8].rearrange("g p c -> p g c"), writes=[dst.b])
            Er = newtmp()
            Ei = newtmp()
            W1 = newtmp()
            W2 = newtmp()
            P.tt("dve", Er.g(), Br.g(), bc(KR), ALU.mult, [Br.b, KR.b], [Er.b])
            P.tt("dve", W1.g(), Bi.g(), bc(KI), ALU.mult, [Bi.b, KI.b], [W1.b])
            P.tt("dve", Er.g(), Er.g(), W1.g(), ALU.subtract, [Er.b, W1.b], [Er.b])
            P.tt("dve", Ei.g(), Bi.g(), bc(KR), ALU.mult, [Bi.b, KR.b], [Ei.b])
            P.tt("dve", W1.g(), Br.g(), bc(KI), ALU.mult, [Br.b, KI.b], [W1.b])
            P.tt("dve", Ei.g(), Ei.g(), W1.g(), ALU.add, [Ei.b, W1.b], [Ei.b])
            rel(KR, KI)
            Fr = Br
            Fi = Bi
            tn = newtmp()
            for dst, src in ((Fr, c_re), (Fi, c_im)):
                for ct in range(4):
                    srcv = src[0][8 * ct:8 * ct + 8].rearrange("g c p -> (g c) p")
                    P.dma("sp", tn.ap[:, ct * 128:ct * 128 + 64], srcv, writes=[tn.b])
                    P.dma("sp", tn.ap[:, ct * 128 + 64:ct * 128 + 128], srcv, writes=[tn.b])
                pb = pring.next()
                for ct in range(4):
                    P.tr(pb.h[:, ct * 128:(ct + 1) * 128], tn.ap[:, ct * 128:(ct + 1) * 128], identf[:], [tn.b, identf.b], [pb.b])
                P.tt("dve", dst.g(), pb.h[:, :].rearrange("p (g c) -> p g c", c=16), bc(PM), ALU.mult, [pb.b, PM.b], [dst.b])
            rel(tn, PM)
            nFi0 = newtmp()
            P.ts("dve", nFi0.ap, Fi.ap, -1.0, None, ALU.mult, None, [Fi.b], [nFi0.b])
            Fr0 = newtmp()
            P.cp("dve", Fr0.ap, Fr.ap, [Fr.b], [Fr0.b])

            def cmul_step(Xr, Xi, Wa, Wb):
                P.tt("dve", Wa.g(), Xr.g(), bc(LBr), ALU.mult, [Xr.b, LBr.b], [Wa.b])
                P.tt("dve", Wb.g(), Xi.g(), bc(LBi), ALU.mult, [Xi.b, LBi.b], [Wb.b])
                P.tt("dve", Wa.g(), Wa.g(), Wb.g(), ALU.subtract, [Wa.b, Wb.b], [Wa.b])
                P.tt("dve", Wb.g(), Xr.g(), bc(LBi), ALU.mult, [Xr.b, LBi.b], [Wb.b])
                P.tt("dve", Xi.g(), Xi.g(), bc(LBr), ALU.mult, [Xi.b, LBr.b], [Xi.b])
                P.tt("dve", Xi.g(), Xi.g(), Wb.g(), ALU.add, [Xi.b, Wb.b], [Xi.b])
                P.cp("dve", Xr.ap, Wa.ap, [Wa.b], [Xr.b])

            for n in range(TCH):
                for ri, E in ((0, Er), (1, Ei)):
                    pb = pring.next()
                    for ct in range(4):
                        P.tr(pb.h[:, ct * 128:(ct + 1) * 128], E.ap[:, ct * 128:(ct + 1) * 128], identf[:], [E.b, identf.b], [pb.b])
                    P.cp("act", Gt[:, :, n, ri, :], pb.h[:, :].rearrange("p (t c) -> p t c", c=128), [pb.b], [Gt.b])
                pb = pring.next()
                for ct in range(4):
                    cs_ = slice(ct * 128, (ct + 1) * 128)
                    P.mm(pb.h[:, cs_], Er.ap[:, cs_], Fr0.ap[:, cs_], True, False, [Er.b, Fr0.b], [pb.b])
                    P.mm(pb.h[:, cs_], Ei.ap[:, cs_], nFi0.ap[:, cs_], False, True, [Ei.b, nFi0.b], [pb.b])
                for ct in range(4):
                    cs_ = slice(ct * 128, (ct + 1) * 128)
                    if n == 0:
                        P.tt("dve", W1.ap[:, 0:128], pb.h[:, cs_], bdm[:], ALU.mult, [pb.b, bdm.b], [W1.b])
                        P.stt("dve", LagK[:, ct, 0, :], identf[:], dcol[:, ct:ct + 1], W1.ap[:, 0:128], ALU.mult, ALU.add,
                              [identf.b, dcol.b, W1.b], [LagK.b])
                    else:
                        P.tt("dve", LagK[:, ct, n, :], pb.h[:, cs_], bdm[:], ALU.mult, [pb.b, bdm.b], [LagK.b])
                if n < TCH - 1:
                    cmul_step(Er, Ei, W1, W2)
            for n in range(1, TCH + 1):
                cmul_step(Fr, Fi, W1, W2)
                P.cp("act", Hst[:, n - 1, 0, :], Fr.ap, [Fr.b], [Hst.b])
                P.ts("dve", Hst[:, n - 1, 1, :], Fi.ap, -1.0, None, ALU.mult, None, [Fi.b], [Hst.b])
            Pr = newtmp()
            Pi = newtmp()
            P.cp("dve", Pr.s(), LBr.s(), [LBr.b], [Pr.b])
            P.cp("dve", Pi.s(), LBi.s(), [LBi.b], [Pi.b])
            for _ in range(3):
                P.tt("dve", W1.s(), Pr.s(), Pr.s(), ALU.mult, [Pr.b], [W1.b])
                P.tt("dve", W2.s(), Pi.s(), Pi.s(), ALU.mult, [Pi.b], [W2.b])
                P.tt("dve", W1.s(), W1.s(), W2.s(), ALU.subtract, [W1.b, W2.b], [W1.b])
                P.tt("dve", W2.s(), Pr.s(), Pi.s(), ALU.mult, [Pr.b, Pi.b], [W2.b])
                P.ts("dve", Pi.s(), W2.s(), 2.0, None, ALU.mult, None, [W2.b], [Pi.b])
                P.cp("dve", Pr.s(), W1.s(), [W1.b], [Pr.b])
            for half in range(2):
                hs = slice(64 * half, 64 * half + 64)
                srcr = Pr.ap[hs, 0:32].rearrange("p (s two) -> p s two", two=2)[:, :, half]
                srci = Pi.ap[hs, 0:32].rearrange("p (s two) -> p s two", two=2)[:, :, half]
                P.cp("dve", CA[hs, 0, :], srcr, [Pr.b], [CA.b])
                P.cp("dve", CA[hs, 1, :], srcr, [Pr.b], [CA.b])
                P.ts("dve", CB[hs, 0, :], srci, -1.0, None, ALU.mult, None, [Pi.b], [CB.b])
                P.cp("dve", CB[hs, 1, :], srci, [Pi.b], [CB.b])

            ckpt("phase1")
            P.barrier()

            v8 = lambda s: s.h[:, :].rearrange("p (k j) -> p k j", j=512)
            v4 = lambda s: s.h[:, 0:2048].rearrange("p (k j) -> p k j", j=512)
            vam = lambda s: s.h[:, 2048:3072].rearrange("p (k j) -> p k j", j=256)
            vdn = lambda s: s.h[:, :].rearrange("p (t j) -> p t j", j=1024)

            WDEPTH = NSLOT - 2

            def block_reqs(is_meta):
                r = [[("in", c, ws_in[c], v8)] for c in (1, 2, 3, 0)]
                for m2 in range(4):
                    r.append([("gm", m2, ws_gm[m2], v8)])
                    r.append([("zm", m2, ws_zm[m2], v4), ("am", m2, ws_am[m2], vam)])
                r += [[("o", hh, ws_o[hh], v8)] for hh in range(2)]
                for g in range(6):
                    nunits = 2 if g < 5 else 1
                    def dnreq(gd):
                        nud = 2 if gd < 5 else 1
                        return [("dn", gd, ws_dn[gd][:, 0:2 * nud, :], (lambda s, nu=nud: vdn(s)[:, 0:2 * nu, :]))]
                    for uu in range(nunits):
                        r.append([("up", 2 * g + uu, ws_up[2 * g + uu], v8)])
                        if not is_meta and uu == 0 and g >= 1:
                            r.append(dnreq(g - 1))
                    if not is_meta and g == 5:
                        r.append(dnreq(5))
                return r

            all_reqs = []
            for _seq in range(NSEQ):
                for _m in ((True, False, False, False, False) if _seq == 0 else (False, False, False, False)):
                    all_reqs += block_reqs(_m)
            wq_state = [0, 0]

            def wnext(expect):
                i = wq_state[0]
                wq_state[0] += 1
                assert all_reqs[i][0][0] == expect[0] and all_reqs[i][0][1] == expect[1], (all_reqs[i][0][:2], expect)
                while wq_state[1] < len(all_reqs) and wq_state[1] <= i + WDEPTH:
                    j = wq_state[1]
                    slot = slots[j % NSLOT]
                    for (name, c, src, vf) in all_reqs[j]:
                        P.dma("sp", vf(slot), src, reads=ensure_cast(name, c), writes=[slot.b])
                    wq_state[1] += 1
                slot = slots[i % NSLOT]
                views = [vf(slot) for (_n, _c, _s, vf) in all_reqs[i]]
                return (slot, views[0]) if len(views) == 1 else (slot, views)

            def rms_stats(srcs, sbufs, ss_t, rstd_t, junk_ap, junk_bufs, tp, nt):
                P.memset(cur_eng[0], ss_t[:], 0.0, [ss_t.b])
                for t in range(nt):
                    P.act(junk_ap[0:tp, :], srcs[t][0:tp, :], AF.Square, [sbufs[t], ss_t.b], junk_bufs + [ss_t.b], accum_out=ss_t[0:tp, t:t + 1])
                P.act(rstd_t[0:tp, 0:nt], ss_t[0:tp, 0:nt], AF.Sqrt, [ss_t.b], [rstd_t.b], scale=1.0 / D, bias=EPS)
                P.op("dve", lambda e: e.reciprocal(out=rstd_t[0:tp, 0:nt], in_=rstd_t[0:tp, 0:nt]), [rstd_t.b], [rstd_t.b])

            def rms_T(srcs, sbufs, rstd_t, tp, nt, gcol, dstT):
                for t in range(nt):
                    xb_ = xnb[t % 2]
                    P.act(xb_[0:tp, :], srcs[t][0:tp, :], AF.Identity, [sbufs[t], rstd_t.b], [xb_.b], scale=rstd_t[0:tp, t:t + 1])
                    pb = pring.next()
                    pv_ = bank_bf16(pb).rearrange("p (k c) -> p k c", c=128)
                    for k in range(KT):
                        P.tr(pv_[:, k, 0:tp], xb_[0:tp, k * 128:(k + 1) * 128], identb[0:tp, 0:tp], [xb_.b, identb.b], [pb.b])
                    P.tt("dve", dstT[:, :, t * 128:t * 128 + tp], pv_[:, :, 0:tp],
                         gcol[:, :].unsqueeze(2).to_broadcast([128, KT, tp]), ALU.mult, [pb.b, gcol.b], [dstT.b])

            def rmsnorm_T(src_tb, n, tp, nt, gcol, dstT):
                srcs = [src_tb[:, t, :] for t in range(nt)]
                sbufs = [src_tb.b] * nt
                rms_stats(srcs, sbufs, ss, rstd, junk, [junk.b], tp, nt)
                rms_T(srcs, sbufs, rstd, tp, nt, gcol, dstT)

            f32v = lambda h_: h_.rearrange("p a b -> p (a b)").bitcast(F32)
            xp = [f32v(zT.h[:, :, :]), f32v(attnT.h[:, :, :]),
                  Sprev.h[:, :, :, :].rearrange("p a b c -> p (a b c)").bitcast(F32), f32v(QA.h[:, 0:4, :])]
            xpb = [zT.b, attnT.b, Sprev.b, QA.b]
            xorder = [(sq_, bi_) for sq_ in range(NSEQ) for bi_ in range(4)]
            prefetched = set()

            zcur = [0]
            cur_eng = ["pool"]

            def scan_steps(j0, j1):
                for j in range(j0, j1):
                    Zc = Zs[zcur[0]]
                    Zn = Zs[1 - zcur[0]]
                    P.cp(cur_eng[0], Sprev[:, :, :, j], Zc[:], [Zc.b], [Sprev.b])
                    P.tt("dve", zt1[:], CA[:], Zc[:], ALU.mult, [CA.b, Zc.b], [zt1.b])
                    P.tt("dve", zt2[:, 0, :], CB[:, 0, :], Zc[:, 1, :], ALU.mult, [CB.b, Zc.b], [zt2.b])
                    P.tt("dve", zt2[:, 1, :], CB[:, 1, :], Zc[:, 0, :], ALU.mult, [CB.b, Zc.b], [zt2.b])
                    P.tt("dve", zt1[:], zt1[:], zt2[:], ALU.add, [zt1.b, zt2.b], [zt1.b])
                    P.tt("dve", Zn[:], zt1[:], Vsb[:, :, :, j], ALU.add, [zt1.b] + R8b, [Zn.b])
                    zcur[0] = 1 - zcur[0]

            for seq in range(NSEQ):
                xblocks = [(NMETA + 512 * i, 512, False, i) for i in range(4)]
                if seq == 0:
                    P.memset("dve", Fcar[:], 0.0, [Fcar.b])
                    P.memset("dve", Zs[zcur[0]][:], 0.0, [Zs[zcur[0]].b])
                    P.memset("dve", halo[:], 0.0, halo_b)
                    blocks = [(0, NMETA, True, 0)] + xblocks
                else:
                    P.cp("pool", Fcar[:], snapF[:], [snapF.b], [Fcar.b])
                    P.cp("pool", Zs[zcur[0]][:], snapZ[:], [snapZ.b], [Zs[zcur[0]].b])
                    P.cp("pool", halo[:], snapH[:], [snapH.b], halo_b)
                    blocks = xblocks
                for (pos0, n, is_meta, bi) in blocks:
                    cur_eng[0] = "dve" if is_meta else "pool"
                    tp = min(n, 128)
                    nt = n // tp
                    nch = n // TCH
                    was_pref = (not is_meta) and ((seq, bi) in prefetched)
                    if was_pref:
                        for t in range(4):
                            P.cp("pool", xtok[:, t, :], xp[t], [xpb[t]], [xtok.b])
                        qav = QA.h[:, 0:4, :].rearrange("p (j h) n -> p j h n", h=2)
                        P.memset("pool", qav[64:128, :, 0, :], 0.0, [QA.b])
                        P.memset("pool", qav[0:64, :, 1, :], 0.0, [QA.b])
                    else:
                        if is_meta:
                            P.dma("sp", xtok[0:NMETA, 0, :], meta_tokens, writes=[xtok.b])
                        else:
                            P.dma("pool", xtok[:, :, :], x[seq, bi * 512:(bi + 1) * 512, :].rearrange("(t p) d -> p t d", p=128), writes=[xtok.b])
                        rmsnorm_T(xtok, n, tp, nt, gmix, xnT)
                    ckpt("s1_%d_%d" % (seq, pos0))
                    sk, wk = wnext(("in", 1))
                    for j in range(4):
                        pb = pring.next()
                        for k in range(KT):
                            P.mm(pb.h[:, 0:n], wk[:, k, j * 128:(j + 1) * 128], xnT[:, k, 0:n], k == 0, k == KT - 1, [sk.b, xnT.b], [pb.b])
                        P.cp("dve", KA[:, j, pos0:pos0 + n], pb.h[:, 0:n], [pb.b], [KA.b])
                    pb = pring.next()
                    for k in range(KT):
                        P.mm(pb.h[0:8, 0:n], wf[:, k, :], xnT[:, k, 0:n], k == 0, k == KT - 1, [wf.b, xnT.b], [pb.b])
                    P.act(fe[:, 0:n], pb.h[0:8, 0:n], AF.Exp, [pb.b, negb.b], [fe.b], scale=-1.0, bias=negb[:, 0:1])
                    P.act(fe[:, 0:n], fe[:, 0:n], AF.Ln, [fe.b], [fe.b], bias=1.0)
                    P.op("dve", lambda e, n=n: e.tensor_tensor_scan(out=Fp[:, 0:n], data0=ones[0:8, 0:n], data1=fe[:, 0:n],
                                                                  initial=Fcar[:, 0:1], op0=ALU.mult, op1=ALU.add),
                         [ones.b, fe.b, Fcar.b], [Fp.b])
                    P.cp("dve", Fcar[:, 0:1], Fp[:, n - 1:n], [Fp.b], [Fcar.b])
                    P.ts("dve", QF[0:8, 0:n], Fp[:, 0:n], -1.0, None, ALU.mult, None, [Fp.b], [QF.b])
                    pb = pring.next()
                    for t in range(nt):
                        P.tr(pb.h[0:tp, 8 * t:8 * t + 8], Fp[:, t * 128:t * 128 + tp], identf[0:8, 0:8], [Fp.b, identf.b], [pb.b])
                    kt0 = 0 if is_meta else 1 + 4 * bi
                    P.cp("dve", FT[0:tp, kt0:kt0 + nt, :], pb.h[0:tp, 0:8 * nt].rearrange("p (t h) -> p t h", h=8), [pb.b], [FT.b])
                    sv, wv = wnext(("in", 2))
                    for t in range(nt):
                        pb = pring.next()
                        for k in range(KT):
                            P.mm(pb.h[0:tp, :], xnT[:, k, t * 128:t * 128 + tp], wv[:, k, :], k == 0, k == KT - 1, [sv.b, xnT.b], [pb.b])
                        dstv = Vc[0:tp, kt0 + t, :, :].rearrange("p j (a c) -> p j a c", c=64)[:, :, 0:3:2, :]
                        P.cp("dve", dstv, pb.h[0:tp, :].rearrange("p (j a c) -> p j a c", a=2, c=64), [pb.b], [Vc.b])
                    su, wu = wnext(("in", 3))
                    for ct in range(4):
                        pb = pring.next()
                        for k in range(KT):
                            P.mm(pb.h[:, 0:n], wu[:, k, ct * 128:(ct + 1) * 128], xnT[:, k, 0:n], k == 0, k == KT - 1, [su.b, xnT.b], [pb.b])
                        P.cp("act", UT[:, ct, 0:n], pb.h[:, 0:n], [pb.b], [UT.b])
                    ckpt("s2_%d_%d" % (seq, pos0))
                    vb = [pring.next() for _ in range(4)]
                    vq = [b_.h[:, :].rearrange("p (r c j) -> p r c j", r=2, c=4) for b_ in vb]
                    for ct in range(4):
                        utv = UT[:, ct, 0:n].rearrange("p (j i) -> p j i", i=TCH)
                        for ri in range(2):
                            for ip in range(TCH):
                                for sl in range(4):
                                    rs_ = slice(32 * sl, 32 * sl + 32)
                                    P.mm(vq[sl][:, ri, ct, 0:nch], Gt[rs_, ct, TCH - 1 - ip, ri, :], utv[rs_, :, ip],
                                         ip == 0, ip == TCH - 1, [Gt.b, UT.b], [vb[sl].b], tile_position=(32 * sl, 0))
                    for sl in range(4):
                        dst = Vsb.ap.rearrange("p r (c s) j -> p r c s j", s=4)[:, :, :, sl, 0:nch]
                        P.cp("act", dst, vq[sl][:, :, :, 0:nch], [vb[sl].b], R8b)
                    sq, wq = wnext(("in", 0))
                    for j in range(4):
                        pb = pring.next()
                        for k in range(KT):
                            P.mm(pb.h[:, 0:n], wq[:, k, j * 128:(j + 1) * 128], xnT[:, k, 0:n], k == 0, k == KT - 1, [sq.b, xnT.b], [pb.b])
                        for half in range(2):
                            rows = slice(64 * half, 64 * half + 64)
                            P.act(QA[rows, 2 * j + half, 0:n], pb.h[rows, 0:n], AF.Copy, [pb.b], [QA.b], scale=0.125)
                    ckpt("s3_%d_%d" % (seq, pos0))
                    if is_meta:
                        ktiles = [(0, NMETA, True, 0)]
                    else:
                        ktiles = [(0, NMETA, False, 0)] + [(1 + t, 128, False, 0) for t in range(4 * bi)] + \
                                 [(1 + 4 * bi + jj, 128, True, 128 * jj) for jj in range(4)]
                    steps_per_head = (nch + H - 1) // H
                    sdone = [0]
                    obs = {}
                    ptc = [0]

                    def emit_S(h, ti):
                        j, half = h // 2, h % 2
                        rows = slice(64 * half, 64 * half + 64)
                        kt, nk, diag, c0 = ktiles[ti]
                        kp0 = 0 if kt == 0 else NMETA + 128 * (kt - 1)
                        sbk = pring.next()
                        P.mm(sbk.h[0:nk, c0:n], KA[:, j, kp0:kp0 + nk], QA[:, h, c0:n], True, False, [KA.b, QA.b], [sbk.b])
                        P.mm(sbk.h[0:nk, c0:n], selb[:, h, 0:nk], QF[:, c0:n], False, not diag, [selb.b, QF.b], [sbk.b])
                        if diag:
                            P.mm(sbk.h[0:nk, c0:c0 + nk], identb[0:nk, 0:nk], maskb[0:nk, 0:nk], False, True, [identb.b, maskb.b], [sbk.b])
                        pt = PT[ptc[0] % 3]
                        ptc[0] += 1
                        P.act(pt[0:nk, c0:n], sbk.h[0:nk, c0:n], AF.Exp, [sbk.b, FT.b], [pt.b], bias=FT[0:nk, kt, h:h + 1], scale=1.0)
                        return pt

                    def emit_PV(h, ti, pt):
                        j, half = h // 2, h % 2
                        rows = slice(64 * half, 64 * half + 64)
                        kt, nk, diag, c0 = ktiles[ti]
                        if ti == 0:
                            obs[h] = oring.next()
                        ob = obs[h]
                        P.mm(ob.h[:, c0:n], Vc[0:nk, kt, j, 64 * half:64 * half + 128], pt[0:nk, c0:n], ti == 0, ti == len(ktiles) - 1,
                             [Vc.b, pt.b], [ob.b])
                        if ti == len(ktiles) - 1:
                            rc = rec[h % 2]
                            srows = slice(64, 128) if half == 0 else slice(0, 64)
                            P.op("dve", lambda e, rc=rc, ob=ob, rows=rows, srows=srows, n=n: e.reciprocal(out=rc[rows, 0:n], in_=ob.h[srows, 0:n]),
                                 [ob.b], [rc.b])
                            P.tt("dve", attnT[rows, j, 0:n], ob.h[rows, 0:n], rc[rows, 0:n], ALU.mult, [ob.b, rc.b], [attnT.b])
                            s1 = min(nch, sdone[0] + steps_per_head)
                            scan_steps(sdone[0], s1)
                            sdone[0] = s1

                    pend = []
                    for h in range(H):
                        for ti in range(len(ktiles)):
                            pt = emit_S(h, ti)
                            pend.append((h, ti, pt))
                            if len(pend) > 2:
                                emit_PV(*pend.pop(0))
                    while pend:
                        emit_PV(*pend.pop(0))
                    scan_steps(sdone[0], nch)
                    ckpt("s4_%d_%d" % (seq, pos0))
                    for ct in range(4):
                        pb = pring.next()
                        Yv = pb.h[:, 0:n].rearrange("p (j i) -> p j i", i=TCH)
                        utv = UT[:, ct, 0:n].rearrange("p (j i) -> p j i", i=TCH)
                        for tau in range(TCH):
                            P.mm(Yv[:, :, tau:TCH], LagK[:, ct, tau, :], utv[:, :, 0:TCH - tau], tau == 0, False, [LagK.b, UT.b], [pb.b])
                        for i in range(TCH):
                            for sl in range(4):
                                for ri in range(2):
                                    last = (i == TCH - 1 and sl == 3 and ri == 1)
                                    P.mm(Yv[32 * sl:32 * sl + 32, :, i], Hst[:, i, ri, 128 * ct + 32 * sl:128 * ct + 32 * sl + 32],
                                         Sprev[:, ri, 4 * ct + sl, 0:nch], False, last, [Hst.b, Sprev.b], [pb.b], tile_position=(0, 32 * sl))
                        P.act(zT[:, ct, 0:n], pb.h[:, 0:n], AF.Gelu, [pb.b], [zT.b])
                    ckpt("s5_%d_%d" % (seq, pos0))
                    for m2 in range(4):
                        sg_, wg = wnext(("gm", m2))
                        sz_, (wz, wa) = wnext(("zm", m2))
                        for mm_ in range(2):
                            m = 2 * m2 + mm_
                            sset = sig[m % 2]
                            c1 = slice(mm_ * 128, mm_ * 128 + 128)
                            c2 = slice(256 + mm_ * 128, 256 + mm_ * 128 + 128)
                            for which, cs_ in ((0, c1), (1, c2)):
                                pb = pring.next()
                                for k in range(KT):
                                    P.mm(pb.h[:, 0:n], wg[:, k, cs_], xnT[:, k, 0:n], k == 0, k == KT - 1, [sg_.b, xnT.b], [pb.b])
                                P.act(sset[which][:, 0:n], pb.h[:, 0:n], AF.Sigmoid, [pb.b], [sset[which].b])
                            pb = pring.next()
                            for k in range(4):
                                P.mm(pb.h[:, 0:n], wz[:, k, c2], zT[:, k, 0:n], k == 0, k == 3, [sz_.b, zT.b], [pb.b])
                            P.act(sset[2][:, 0:n], pb.h[:, 0:n], AF.Sigmoid, [pb.b], [sset[2].b])
                            pb = pring.next()
                            for k in range(4):
                                P.mm(pb.h[:, 0:n], wz[:, k, c1], zT[:, k, 0:n], k == 0, k == 3, [sz_.b, zT.b], [pb.b])
                            P.tt("dve", ysb[:, 0:n], pb.h[:, 0:n], sset[2][:, 0:n], ALU.mult, [pb.b, sset[2].b], [ysb.b])
                            P.tt(cur_eng[0], t2b[:, 0:n], ysb[:, 0:n], sset[1][:, 0:n], ALU.mult, [ysb.b, sset[1].b], [t2b.b])
                            pb = pring.next()
                            for k in range(4):
                                P.mm(pb.h[:, 0:n], wa[:, k, c1], attnT[:, k, 0:n], k == 0, k == 3, [sz_.b, attnT.b], [pb.b])
                            P.tt("dve", t1b[:, 0:n], pb.h[:, 0:n], sset[0][:, 0:n], ALU.mult, [pb.b, sset[0].b], [t1b.b])
                            P.tt(cur_eng[0], mergedT[:, m, 0:n], t1b[:, 0:n], t2b[:, 0:n], ALU.add, [t1b.b, t2b.b], R8b)
                    for hh in range(2):
                        so, wo = wnext(("o", hh))
                        for t in range(nt):
                            pb = pring.next()
                            for k in range(KT):
                                P.mm(pb.h[0:tp, :], mergedT[:, k, t * 128:t * 128 + tp], wo[:, k, :], k == 0, k == KT - 1, [so.b] + R8b, [pb.b])
                            dsth = xtok[0:tp, t, hh * 512:(hh + 1) * 512]
                            P.tt("dve", dsth, dsth, pb.h[0:tp, :], ALU.add, [pb.b, xtok.b], [xtok.b])
                    ckpt("s6_%d_%d" % (seq, pos0))
                    rmsnorm_T(xtok, n, tp, nt, gffn, xn2T)

                    def ffn_down(g, n=n, tp=tp, nt=nt):
                        nu = 2 if g < 5 else 1
                        at = actT[g % 2]
                        sd, wd = wnext(("dn", g))
                        for t in range(nt):
                            for hh in range(2):
                                pb = pring.next()
                                for il in range(2 * nu):
                                    P.mm(pb.h[0:tp, :], at[:, il, t * 128:t * 128 + tp], wd[:, il, hh * 512:(hh + 1) * 512],
                                         il == 0, il == 2 * nu - 1, [at.b, sd.b], [pb.b])
                                dsth = xtok[0:tp, t, hh * 512:(hh + 1) * 512]
                                P.tt("dve", dsth, dsth, pb.h[0:tp, :], ALU.add, [pb.b, xtok.b], [xtok.b])

                    nxt = None
                    if not is_meta:
                        k_ = xorder.index((seq, bi))
                        if k_ + 1 < len(xorder):
                            nxt = xorder[k_ + 1]
                    if nxt is not None:
                        for t in range(4):
                            P.dma("pool", xp[t], x[nxt[0], nxt[1] * 512 + t * 128:nxt[1] * 512 + (t + 1) * 128, :], writes=[xpb[t]])
                    pend_conv = []
                    for g in range(6):
                        nunits = 2 if g < 5 else 1
                        at = actT[g % 2]
                        if nxt is not None and g == 2:
                            rms_stats(xp, xpb, ss2, rstd2, xnb[1][:, :], [xnb[1].b], 128, 4)
                        for uu in range(nunits):
                            c = 2 * g + uu
                            sup, wup = wnext(("up", c))
                            for e_ in range(2):
                                i = 2 * c + e_
                                il = 2 * uu + e_
                                parts = ((0, slice(e_ * 128, e_ * 128 + 128), i, cv[i % 2]),
                                         (1, slice(256 + e_ * 128, 256 + e_ * 128 + 128), NFT + i, cg[i % 2]))
                                for which, cs_, ci, dst in parts:
                                    pb = pring.next()
                                    for k in range(KT):
                                        P.mm(pb.h[:, 0:n], wup[:, k, cs_], xn2T[:, k, 0:n], k == 0, k == KT - 1, [sup.b, xn2T.b], [pb.b])
                                    hbuf = hb[which][i % 2]
                                    P.cp(cur_eng[0], hbuf[:, 0:2], halo[:, ci, :], [halo_b[ci]], [hbuf.b])
                                    P.cp("dve", hbuf[:, 2:2 + n], pb.h[:, 0:n], [pb.b], [hbuf.b])
                                    P.cp(cur_eng[0], halo[:, ci, :], hbuf[:, n:n + 2], [hbuf.b], [halo_b[ci]])
                                    if not is_meta:
                                        dgs = dg[i % 2][which]
                                        for tap in range(3):
                                            P.ts("dve", dgs[tap][:], identb[:], cwb[:, ci, tap:tap + 1], None, ALU.mult, None,
                                                 [identb.b, cwb.b], [dgs[tap].b])

                                        def conv_part(which=which, ci=ci, dst=dst, hbuf=hbuf, dgs=dgs, i=i, il=il, at=at, n=n):
                                            pc = pring.next()
                                            for tap in range(3):
                                                P.mm(pc.h[:, 0:n], dgs[tap][:], hbuf[:, tap:tap + n], tap == 0, tap == 2,
                                                     [dgs[tap].b, hbuf.b], [pc.b])
                                            if which == 0:
                                                P.act(dst[:, 0:n], pc.h[:, 0:n], AF.Identity, [pc.b, cwb.b], [dst.b], bias=cwb[:, ci, 3:4])
                                            else:
                                                P.act(sgb[:, 0:n], pc.h[:, 0:n], AF.Silu, [pc.b, cwb.b], [sgb.b], bias=cwb[:, ci, 3:4])
                                                P.tt("pool", at[:, il, 0:n], sgb[:, 0:n], cv[i % 2][:, 0:n], ALU.mult, [sgb.b, cv[i % 2].b], [at.b])

                                        pend_conv.append(conv_part)
                                        if len(pend_conv) > 1:
                                            pend_conv.pop(0)()
                            if not is_meta and uu == 0 and g >= 1:
                                ffn_down(g - 1)
                        if not is_meta and g == 5:
                            while pend_conv:
                                pend_conv.pop(0)()
                            ffn_down(5)
                            if nxt is not None:
                                rms_T(xp, xpb, rstd2, 128, 4, gmix, xnT)
                                prefetched.add(nxt)
                    ckpt("s7_%d_%d" % (seq, pos0))
                    if is_meta:
                        P.cp("dve", snapF[:], Fcar[:], [Fcar.b], [snapF.b])
                        P.cp("dve", snapZ[:], Zs[zcur[0]][:], [Zs[zcur[0]].b], [snapZ.b])
                        P.cp("dve", snapH[:], halo[:], halo_b, [snapH.b])
                    if not is_meta:
                        P.memset("pool", ss[:], 0.0, [ss.b])
                        for t in range(nt):
                            P.act(junk[0:tp, :], xtok[0:tp, t, :], AF.Square, [xtok.b, ss.b], [junk.b, ss.b], accum_out=ss[0:tp, t:t + 1])
                        P.act(rstd[0:tp, 0:nt], ss[0:tp, 0:nt], AF.Sqrt, [ss.b], [rstd.b], scale=1.0 / D, bias=EPS)
                        P.op("dve", lambda e: e.reciprocal(out=rstd[0:128, 0:4], in_=rstd[0:128, 0:4]), [rstd.b], [rstd.b])
                        for t in range(nt):
                            P.stt("dve", xtok[0:tp, t, :], xtok[0:tp, t, :], rstd[0:tp, t:t + 1], gfin[0:tp, :], ALU.mult, ALU.mult,
                                  [xtok.b, rstd.b, gfin.b], [xtok.b])
                        P.dma("pool", y[seq, bi * 512:(bi + 1) * 512, :].rearrange("(t p) d -> p t d", p=128), xtok[:, :, :], reads=[xtok.b], writes=[Buf()])

        except _Stop:
            pass
        SBUF_LEFT.append(nc.sbuf_bytes_remaining)
        P.barrier()
        P.emit()
    return nc


_CACHE = {}


def _consts():
    ident = np.eye(128, dtype=np.float32)
    k = np.arange(128)[:, None]
    q = np.arange(128)[None, :]
    mask = np.where(k <= q, 0.0, MASKV).astype(np.float32)
    sel = np.zeros((8, 8, 128), dtype=np.float32)
    for h in range(8):
        sel[h, h, :] = 1.0
    bd = (k // 16 == q // 16).astype(np.float32)
    return {"c_ident": ident, "c_mask": mask, "c_sel": sel, "c_bd": bd}


def kernel(**inputs):
    if "nc" not in _CACHE:
        _CACHE["nc"] = build_program()
    nc = _CACHE["nc"]
    consts = _consts()
    ncores = 8
    in_maps = []
    for c in range(ncores):
        m = {}
        for k_, v in inputs.items():
            a = np.ascontiguousarray(np.asarray(v, dtype=np.float32))
            if k_ == "x":
                a = np.ascontiguousarray(a[NSEQ * c:NSEQ * (c + 1)])
            m[k_] = a
        m.update(consts)
        in_maps.append(m)
    res = run_bass_kernel_spmd(nc, in_maps, core_ids=list(range(ncores)))
    out = np.concatenate([np.asarray(r["y"], dtype=np.float32) for r in res.results], axis=0)
    return out
```
